# Optimizing a Trainium2 kernel written in Bass

```python
import jax, jax.numpy as jnp
from jax import lax
import numpy as np

D_MODEL = 1024
BATCH = 16
SEQ = 4096
DEPTH = 2
DEC_BATCH = 16
DEC_SEQ = 16
PAST_LEN = 2048

CHUNK = 64
A_WIDTH = 256
A_GROUPS = 4
A_GDIM = A_WIDTH // A_GROUPS
A_CHUNK = 128
B_HEADS = 4
B_DK = 64
B_DV = 64
B_KWIDTH = B_HEADS * B_DK
B_WIDTH = B_HEADS * B_DV
C_HEADS = 8
C_KV_HEADS = 2
C_REP = C_HEADS // C_KV_HEADS
C_HEAD_DIM = 64
C_WIDTH = C_HEADS * C_HEAD_DIM
C_KV_WIDTH = C_KV_HEADS * C_HEAD_DIM
WINDOW = 128
WIN_CHUNKS = WINDOW // CHUNK
CACHE_ROWS = min(WINDOW, PAST_LEN)
ROPE_DIM = C_HEAD_DIM // 4
ROPE_THETA = 500000.0
MIX_WIDTH = A_WIDTH + B_WIDTH + C_WIDTH
IN_SIZES = (A_WIDTH, A_WIDTH, A_WIDTH, B_KWIDTH, B_KWIDTH, B_WIDTH, B_WIDTH, C_WIDTH, C_KV_WIDTH, C_KV_WIDTH, C_WIDTH)
IN_WIDTH = sum(IN_SIZES)
DEEPNORM_ALPHA = (2 * DEPTH) ** 0.25
DEEPNORM_BETA = (8 * DEPTH) ** -0.25
LN_EPS = 1e-5
RMS_EPS = 1e-6

kernel_name = "hybrid_stream_gmlp_hgrn2_swa_step"


def layer_norm(x, g, b):
    xf = x.astype(jnp.float32)
    mu = xf.mean(-1, keepdims=True)
    var = jnp.square(xf - mu).mean(-1, keepdims=True)
    return ((xf - mu) * lax.rsqrt(var + LN_EPS) * g.astype(jnp.float32) + b.astype(jnp.float32)).astype(x.dtype)


def rope_partial(x, pos):
    half = ROPE_DIM // 2
    inv = jnp.power(jnp.float32(ROPE_THETA), -jnp.arange(0, ROPE_DIM, 2, dtype=jnp.float32) / ROPE_DIM)
    ang = pos.astype(jnp.float32)[:, None] * inv[None, :]
    cos = jnp.cos(ang)[None, :, None, :]
    sin = jnp.sin(ang)[None, :, None, :]
    xr = x[..., :ROPE_DIM].astype(jnp.float32)
    x1, x2 = xr[..., :half], xr[..., half:]
    rot = jnp.concatenate([x1 * cos - x2 * sin, x2 * cos + x1 * sin], axis=-1)
    return jnp.concatenate([rot.astype(x.dtype), x[..., ROPE_DIM:]], axis=-1)


def sgu(u, v, ln_g, ln_b, w_s, b_s):
    B_, T, _ = v.shape
    vn = layer_norm(v, ln_g, ln_b)
    L = min(T, A_CHUNK)
    nc = T // L
    vg = vn.reshape(B_, nc, L, A_GROUPS, A_GDIM)
    idx = jnp.arange(L)
    mask = (idx[None, :] // CHUNK) <= (idx[:, None] // CHUNK)
    ws = jnp.where(mask[None], w_s[:, :L, :L], jnp.zeros((), w_s.dtype))
    mixed = jnp.einsum('gij,bcjgd->bcigd', ws, vg) + b_s[:, :L].T[None, None, :, :, None]
    return u * mixed.reshape(B_, T, A_WIDTH).astype(u.dtype), vn


def hgrn2_block(q, k, v, logf, S):
    L = q.shape[1]
    G = jnp.cumsum(logf, axis=1)
    causal = jnp.tril(jnp.ones((L, L), dtype=bool))
    diff = G[:, :, None] - G[:, None, :]
    decay = jnp.exp(jnp.where(causal[None, :, :, None, None], diff, -jnp.inf))
    A = jnp.einsum('bihk,bjhk,bijhk->bhij', q, k, decay)
    o = jnp.einsum('bhij,bjhv->bihv', A, v) + jnp.einsum('bihk,bhkv->bihv', q * jnp.exp(G), S)
    GL = G[:, -1]
    S_new = jnp.exp(GL)[..., None] * S + jnp.einsum('bjhk,bjhv->bhkv', k * jnp.exp(GL[:, None] - G), v)
    return o, S_new


def hgrn2_scan(q, k, v, logf):
    B_, T, H, _ = q.shape
    nc = T // CHUNK

    def to_chunks(a):
        return a.reshape(B_, nc, CHUNK, H, a.shape[-1]).swapaxes(0, 1)

    def step(S, blk):
        qc, kc, vc, gc = blk
        o, S = hgrn2_block(qc, kc, vc, gc, S)
        return S, o

    S0 = jnp.zeros((B_, H, B_DK, B_DV), jnp.float32)
    S_T, o = lax.scan(step, S0, (to_chunks(q), to_chunks(k), to_chunks(v), to_chunks(logf)))
    return o.swapaxes(0, 1).reshape(B_, T, H, B_DV), S_T


def sink_attention(q, k, v, sinks, key_valid):
    B_, C, Q, H, D = q.shape
    qg = q.reshape(B_, C, Q, C_KV_HEADS, C_REP, D).astype(jnp.float32)
    s = jnp.einsum('bcqgrd,bckgd->bcgrqk', qg, k.astype(jnp.float32)) * (D ** -0.5)
    if key_valid is not None:
        s = jnp.where(key_valid[None, :, None, None, None, :], s, -jnp.inf)
    sink = jnp.broadcast_to(sinks.astype(jnp.float32).reshape(1, 1, C_KV_HEADS, C_REP, 1, 1), s.shape[:-1] + (1,))
    p = jax.nn.softmax(jnp.concatenate([s, sink], axis=-1), axis=-1)[..., :-1]
    o = jnp.einsum('bcgrqk,bckgd->bcqgrd', p, v.astype(jnp.float32))
    return o.reshape(B_, C, Q, H * D)


def swa_prompt(q, k, v, sinks):
    B_, T, H, D = q.shape
    nc = T // CHUNK
    pad = WIN_CHUNKS * CHUNK
    kp = jnp.pad(k, ((0, 0), (pad, 0), (0, 0), (0, 0)))
    vp = jnp.pad(v, ((0, 0), (pad, 0), (0, 0), (0, 0)))

    def band(a):
        return jnp.concatenate([a[:, w * CHUNK: w * CHUNK + T].reshape(B_, nc, CHUNK, C_KV_HEADS, D)
                                for w in range(WIN_CHUNKS + 1)], axis=2)

    valid = (jnp.arange(nc)[:, None] + jnp.arange(WIN_CHUNKS + 1)[None, :]) >= WIN_CHUNKS
    valid = jnp.repeat(valid, CHUNK, axis=1)
    o = sink_attention(q.reshape(B_, nc, CHUNK, H, D), band(kp), band(vp), sinks, valid)
    return o.reshape(B_, T, H * D)


def trunk_layer(x, pos, w_in, ln_v_g, ln_v_b, w_s, b_s, lb, norm_b_g, sinks, w_out, ln_g, ln_b,
                state=None, cache=None):
    B_, T, _ = x.shape
    h = x @ w_in
    split_pts = np.cumsum(IN_SIZES)[:-1].tolist()
    uA, vA, zA, qB, fB, iB, zB, qC, kC, vC, zC = jnp.split(h, split_pts, axis=-1)

    yA, vA_rows = sgu(uA, vA, ln_v_g, ln_v_b, w_s, b_s)
    yA = yA * jax.nn.silu(zA)

    f = lb + (1.0 - lb) * jax.nn.sigmoid(fB.astype(jnp.float32))
    logf = jnp.log(f).reshape(B_, T, B_HEADS, B_DK)
    kh = (1.0 - f).reshape(B_, T, B_HEADS, B_DK)
    qh = jax.nn.silu(qB.astype(jnp.float32)).reshape(B_, T, B_HEADS, B_DK)
    vh = iB.astype(jnp.float32).reshape(B_, T, B_HEADS, B_DV)
    if state is None:
        oB, S_new = hgrn2_scan(qh, kh, vh, logf)
    else:
        oB, S_new = hgrn2_block(qh, kh, vh, logf, state.astype(jnp.float32))
    oB = oB * lax.rsqrt(jnp.mean(jnp.square(oB), axis=-1, keepdims=True) + RMS_EPS) * norm_b_g.astype(jnp.float32)
    yB = oB.reshape(B_, T, B_WIDTH).astype(x.dtype) * jax.nn.silu(zB)

    q = rope_partial(qC.reshape(B_, T, C_HEADS, C_HEAD_DIM), pos)
    k = rope_partial(kC.reshape(B_, T, C_KV_HEADS, C_HEAD_DIM), pos)
    v = vC.reshape(B_, T, C_KV_HEADS, C_HEAD_DIM)
    if cache is None:
        oC = swa_prompt(q, k, v, sinks)
        k_keep, v_keep = k[:, -CACHE_ROWS:], v[:, -CACHE_ROWS:]
    else:
        ck, cv = cache
        k_all = jnp.concatenate([ck.astype(k.dtype), k], axis=1)
        v_all = jnp.concatenate([cv.astype(v.dtype), v], axis=1)
        oC = sink_attention(q[:, None], k_all[:, None], v_all[:, None], sinks, None)[:, 0]
        k_keep, v_keep = k, v
    yC = oC.astype(x.dtype) * jax.nn.silu(zC)

    mix = jnp.concatenate([yA, yB, yC], axis=-1)
    out = layer_norm(DEEPNORM_ALPHA * x + mix @ w_out, ln_g, ln_b)
    return out, vA_rows, S_new.astype(x.dtype), k_keep, v_keep


def setup_inputs(seed: int = 0) -> dict:
    key = jax.random.key(seed)
    ks = jax.random.split(key, 16)
    nrm = jax.random.normal
    f32 = jnp.float32
    return {
        "x_prompt": nrm(ks[0], (BATCH, SEQ, D_MODEL), f32),
        "x_sample": nrm(ks[1], (DEC_BATCH, DEC_SEQ, D_MODEL), f32),
        "cache_k": nrm(ks[2], (DEPTH, DEC_BATCH, CACHE_ROWS, C_KV_HEADS, C_HEAD_DIM), f32),
        "cache_v": nrm(ks[3], (DEPTH, DEC_BATCH, CACHE_ROWS, C_KV_HEADS, C_HEAD_DIM), f32),
        "state_hgrn": 0.5 * nrm(ks[4], (DEPTH, DEC_BATCH, B_HEADS, B_DK, B_DV), f32),
        "w_in": nrm(ks[5], (DEPTH, D_MODEL, IN_WIDTH), f32) * D_MODEL ** -0.5,
        "ln_v_g": 1.0 + 0.02 * nrm(ks[6], (DEPTH, A_WIDTH), f32),
        "ln_v_b": 0.02 * nrm(ks[7], (DEPTH, A_WIDTH), f32),
        "w_s": nrm(ks[8], (DEPTH, A_GROUPS, A_CHUNK, A_CHUNK), f32) * A_CHUNK ** -0.5,
        "b_s": 1.0 + 0.02 * nrm(ks[9], (DEPTH, A_GROUPS, A_CHUNK), f32),
        "lb_param": 0.1 * nrm(ks[10], (DEPTH, B_KWIDTH), f32),
        "norm_b_g": 1.0 + 0.02 * nrm(ks[11], (DEPTH, B_DV), f32),
        "sinks": 0.5 * nrm(ks[12], (DEPTH, C_HEADS), f32),
        "w_out": nrm(ks[13], (DEPTH, MIX_WIDTH, D_MODEL), f32) * (MIX_WIDTH ** -0.5) * DEEPNORM_BETA,
        "ln_g": 1.0 + 0.02 * nrm(ks[14], (DEPTH, D_MODEL), f32),
        "ln_b": 0.02 * nrm(ks[15], (DEPTH, D_MODEL), f32),
    }


def reference(x_prompt, x_sample, cache_k, cache_v, state_hgrn, w_in, ln_v_g, ln_v_b, w_s, b_s,
              lb_param, norm_b_g, sinks, w_out, ln_g, ln_b):
    lb_soft = jax.nn.softmax(lb_param.astype(jnp.float32), axis=0)
    lbs = jnp.cumsum(lb_soft, axis=0) - lb_soft[0]
    pos_p = jnp.arange(x_prompt.shape[1], dtype=jnp.int32)
    pos_s = PAST_LEN + jnp.arange(x_sample.shape[1], dtype=jnp.int32)
    yp, ys = x_prompt, x_sample
    kp_l, vp_l, Sp_l, ks_l, vs_l, Ss_l, va_l = [], [], [], [], [], [], []
    for l in range(DEPTH):
        yp, _, Sp, kp, vp = trunk_layer(yp, pos_p, w_in[l], ln_v_g[l], ln_v_b[l], w_s[l], b_s[l], lbs[l],
                                        norm_b_g[l], sinks[l], w_out[l], ln_g[l], ln_b[l])
        ys, va, Ss, kk, vv = trunk_layer(ys, pos_s, w_in[l], ln_v_g[l], ln_v_b[l], w_s[l], b_s[l], lbs[l],
                                         norm_b_g[l], sinks[l], w_out[l], ln_g[l], ln_b[l],
                                         state=state_hgrn[l], cache=(cache_k[l], cache_v[l]))
        kp_l.append(kp); vp_l.append(vp); Sp_l.append(Sp)
        ks_l.append(kk); vs_l.append(vv); Ss_l.append(Ss); va_l.append(va)
    return (yp, ys, jnp.stack(kp_l), jnp.stack(vp_l), jnp.stack(Sp_l),
            jnp.stack(ks_l), jnp.stack(vs_l), jnp.stack(Ss_l), jnp.stack(va_l))
```

```python
import numpy as np
from contextlib import ExitStack
import concourse.bass as bass
import concourse.mybir as mybir
from concourse.bass_utils import run_bass_kernel_spmd

F32 = mybir.dt.float32
BF16 = mybir.dt.bfloat16
ALU = mybir.AluOpType
AF = mybir.ActivationFunctionType
AX = mybir.AxisListType.X

D = 1024
INW = 3072
NL = 2
ALPHA = float((2 * NL) ** 0.25)
LN_EPS = 1e-5
RMS_EPS = 1e-6
PAST = 2048
NCORES = 8
import os
MAX_UNITS = int(os.environ.get("KMAXU", "100000"))
KSTOP = int(os.environ.get("KSTOP", "1000"))

O_ZA, O_ZB, O_ZC, O_U, O_V, O_QC, O_KC, O_VC, O_IB, O_QB, O_FB = 0, 256, 512, 1024, 1280, 1536, 2048, 2176, 2304, 2560, 2816
W_SEGS = [(512, 0, 256), (1536, 256, 256), (2560, 512, 512), (0, 1024, 512), (1792, 1536, 512),
          (2304, 2048, 256), (1280, 2304, 256), (768, 2560, 512)]


class _Op:
    __slots__ = ("idx", "eng", "fn", "chan", "deps", "signal", "ev", "dma", "rows", "pe_self")

    def __init__(self, idx, eng, fn, chan, rows=None):
        self.idx = idx
        self.eng = eng
        self.fn = fn
        self.chan = chan
        self.dma = chan is not None
        self.deps = ()
        self.signal = False
        self.ev = None
        self.rows = rows
        self.pe_self = set()


class Sched:
    ENGS = ("pe", "act", "dve", "pool", "sp")

    def __init__(self):
        self.ops = []
        self.res = {}
        self.bank_last = {}
        self.last_dve = None
        self.pending_pool = None

    def add(self, eng, fn, r=(), w=(), chan=None, rows=None):
        idx = len(self.ops)
        op = _Op(idx, eng, fn, chan, rows)
        deps = set()
        bset = set()
        for R in list(r) + list(w):
            if len(R) >= 2 and R[0] == "B" and R[1].isdigit():
                bset.add(R[:2])
        for b in bset:
            lb = self.bank_last.setdefault(b, {})
            for e2, i2 in lb.items():
                if e2 != eng:
                    deps.add(i2)
                elif eng == "pe":
                    r1 = self.ops[i2].rows or (0, 128)
                    r2 = rows or (0, 128)
                    if r1[1] <= r2[0] or r2[1] <= r1[0]:
                        deps.add(i2)
                        op.pe_self.add(i2)
            lb[eng] = idx
        if eng == "pool" and chan is None and fn is not None:
            if self.last_dve is not None:
                deps.add(self.last_dve)
            self.pending_pool = idx
        if eng == "dve":
            if self.pending_pool is not None:
                deps.add(self.pending_pool)
                self.pending_pool = None
            self.last_dve = idx
        for R in r:
            st = self.res.get(R)
            if st is None:
                st = self.res[R] = [[], []]
            deps.update(st[0])
        for R in w:
            st = self.res.get(R)
            if st is None:
                st = self.res[R] = [[], []]
            deps.update(st[0])
            deps.update(st[1])
        for R in r:
            self.res[R][1].append(idx)
        for R in w:
            st = self.res[R]
            if st[1]:
                st[0] = [idx]
                st[1] = []
            else:
                st[0].append(idx)
        op.deps = deps
        self.ops.append(op)
        return idx

    def emit(self, nc, stack):
        ops = self.ops
        need = [[] for _ in ops]
        for op in ops:
            for d in sorted(op.deps):
                a = ops[d]
                if a.eng == "pe" and op.eng == "pe" and not a.dma and not op.dma and d not in op.pe_self:
                    continue
                a.signal = True
                need[op.idx].append(d)
        sems = {}
        for e in self.ENGS:
            sems[e] = stack.enter_context(nc.semaphore("s_" + e))
        chans = {}
        cnt = {e: 0 for e in self.ENGS}
        ccnt = {}
        for op in ops:
            if op.dma:
                if op.chan not in chans:
                    chans[op.chan] = stack.enter_context(nc.semaphore("c_" + str(op.chan)))
                    ccnt[op.chan] = 0
                ccnt[op.chan] += 16
                op.ev = (op.chan, ccnt[op.chan])
            elif op.signal and op.fn is not None:
                cnt[op.eng] += 1
                op.ev = (op.eng, cnt[op.eng])
        allsem = dict(sems)
        allsem.update(chans)
        per_eng = {e: [op for op in ops if op.eng == e] for e in self.ENGS}
        nwaits = {e: 0 for e in self.ENGS}

        def run(e, eng):
            seen = {}
            for op in per_eng[e]:
                for d in need[op.idx]:
                    a = ops[d]
                    if a.ev is None:
                        continue
                    s, v = a.ev
                    if seen.get(s, 0) >= v:
                        continue
                    seen[s] = v
                    eng.wait_ge(allsem[s], v)
                    nwaits[e] += 1
                if op.fn is None:
                    continue
                ins = op.fn(eng)
                if op.dma:
                    ins.then_inc(chans[op.chan], 16)
                elif op.signal:
                    ins.then_inc(sems[e], 1)

        block = stack.enter_context(nc.Block())

        @block.tensor
        def _(eng):
            run("pe", eng)

        @block.scalar
        def _(eng):
            run("act", eng)

        @block.vector
        def _(eng):
            run("dve", eng)

        @block.gpsimd
        def _(eng):
            run("pool", eng)

        @block.sync
        def _(eng):
            run("sp", eng)

        self.stats = dict(nops=len(ops), cnt=cnt, nwaits=nwaits, nchan=len(chans))


def build(NT):
    nc = bass.Bass("TRN2", target_bir_lowering=False)
    T = NT * 128

    def din(name, shape):
        return nc.dram_tensor(name, shape, F32, kind="ExternalInput").ap()

    def dout(name, shape):
        return nc.dram_tensor(name, shape, F32, kind="ExternalOutput").ap()

    xp = din("xp", [2, T, D])
    xs = din("xs", [2, 16, D])
    ck = din("ck", [NL, 2, 128, 128])
    cv = din("cv", [NL, 2, 128, 128])
    st_in = din("st", [NL, 2, 4, 64, 64])
    w_in = din("w_in", [NL, D, INW])
    w_out = din("w_out", [NL, D, D])
    lnvg = din("lnvg", [NL, 256])
    lnvb = din("lnvb", [NL, 256])
    w_s = din("w_s", [NL, 4, 128, 128])
    b_s = din("b_s", [NL, 4, 128])
    lbp = din("lbp", [NL, 256])
    nbg = din("nbg", [NL, 64])
    sinks = din("sinks", [NL, 8])
    lng = din("lng", [NL, D])
    lnb = din("lnb", [NL, D])
    cosp = din("cosp", [128, NT, 8])
    sinp = din("sinp", [128, NT, 8])
    coss = din("coss", [16, 8])
    sins = din("sins", [16, 8])
    mask2_d = din("mask2", [128, 64])
    smask_d = din("smask", [128, 128])

    yp = dout("yp", [2, T, D])
    ys = dout("ys", [2, 16, D])
    nkp = dout("nkp", [NL, 2, 128, 128])
    nvp = dout("nvp", [NL, 2, 128, 128])
    nhp = dout("nhp", [NL, 2, 4, 64, 64])
    nks = dout("nks", [NL, 2, 16, 128])
    nvs = dout("nvs", [NL, 2, 16, 128])
    nhs = dout("nhs", [NL, 2, 4, 64, 64])
    nva = dout("nva", [NL, 2, 16, 256])

    KDBG = os.environ.get("KDBG", "")
    dbg = None
    if KDBG:
        dbg = dict(mix=nc.dram_tensor("dbg_mix", [128, 1024], BF16, kind="ExternalOutput").ap(),
                   res=dout("dbg_res", [128, 1024]), thg=dout("dbg_thg", [128, 1024]),
                   sel=tuple(int(v) for v in KDBG.split(",")))
    S = Sched()
    finals = []
    uid = [0]

    with ExitStack() as stack:
        def sb(name, shape, dt=F32):
            return stack.enter_context(nc.sbuf_tensor("sb_" + name, shape, dt))

        banks = [stack.enter_context(nc.psum_tensor("bk%d" % i, [128, 512], F32)) for i in range(8)]

        def bkf(i):
            return banks[i][:]

        def bkb(i):
            return banks[i][:].bitcast(BF16)

        Win = sb("Win", [128, NL, 8, INW], BF16)
        Wout = sb("Wout", [128, NL, 8, D], BF16)
        ident = sb("ident", [128, 128], BF16)
        mask2 = sb("mask2", [128, 64], F32)
        wsT = sb("wsT", [128, NL, 4, 128], BF16)
        lnvg_bc = sb("lnvg_bc", [128, NL, 256])
        lnvb_bc = sb("lnvb_bc", [128, NL, 256])
        lng_bc = sb("lng_bc", [128, NL, D])
        lnb_bc = sb("lnb_bc", [128, NL, D])
        nbg_bc = sb("nbg_bc", [128, NL, 64])
        bsT = sb("bsT", [128, NL, 4])
        esink = sb("esink", [128, NL, 8])
        lbt = sb("lbt", [128, NL, 2])
        bco = sb("bco", [128, NL, 2])
        nbco = sb("nbco", [128, NL, 2])
        cosP = sb("cosP", [128, NT, 8])
        sinP = sb("sinP", [128, NT, 8])
        cosS = sb("cosS", [16, 8])
        sinS = sb("sinS", [16, 8])
        mhalf = sb("mhalf", [128, 4])
        ones_c = sb("ones_c", [128, 128])
        m1seg = sb("m1seg", [128, 32])

        xf = [sb("xf%d" % i, [128, D]) for i in range(2)]
        xb = [sb("xb%d" % i, [128, D], BF16) for i in range(2)]
        xT = sb("xT", [128, 8, 128], BF16)
        thg = sb("thg", [128, 1024])
        uf = sb("uf", [128, 256])
        st6 = sb("st6", [128, 2, 6])
        mv = sb("mv", [128, 2])
        veps = sb("veps", [128, 1])
        rstd = sb("rstd", [128, 1])
        vn0 = sb("vn0", [128, 256])
        vnb = sb("vnb", [128, 256], BF16)
        qb = sb("qb", [128, 512], BF16)
        rt = sb("rt", [128, 4, 8, 8])
        kf = sb("kf", [128, 128])
        vf = sb("vf", [128, 128])
        kdup = sb("kdup", [128, 2, 2, 64], BF16)
        NKV = 3
        vaug = [sb("vaug%d" % i, [128, 2, 65], BF16) for i in range(NKV * NL)]
        kT2 = [sb("kT2_%d" % i, [128, 2, 128], BF16) for i in range(NKV * NL)]
        qT = sb("qT", [128, 4, 128], BF16)
        pT = sb("pT", [128, 2, 8, 128], BF16)
        vb = sb("vb", [128, 256], BF16)
        thfq = sb("thfq", [128, 4, 128])
        Pf = sb("Pf", [128, 2, 128])
        Rf = sb("Rf", [128, 2, 128])
        scl = sb("scl", [128, 2, 2, 4])
        qt = sb("qt", [128, 2, 128], BF16)
        kt = sb("kt", [128, 2, 128], BF16)
        At = sb("At", [128, 4, 64], BF16)
        ktok = sb("ktok", [128, 256], BF16)
        Sp = sb("Sp", [128, 2, 2, 64], BF16)
        kvt = sb("kvt", [128, 2, 64])
        ss = sb("ss", [128, 4])
        ss2 = sb("ss2", [128, 4])
        rs = sb("rs", [128, 4])
        t1 = sb("t1", [128, 256])
        gB = sb("gB", [128, 256])
        ta = sb("ta", [128, 256])
        den = sb("den", [128, 8])
        rden = sb("rden", [128, 8])
        mix = sb("mix", [128, 1024], BF16)
        res = sb("res", [128, D])
        st12 = sb("st12", [128, 2, 6])
        mv2 = sb("mv2", [128, 2])
        veps2 = sb("veps2", [128, 1])
        rstd2 = sb("rstd2", [128, 1])
        nmr = sb("nmr", [128, 1])
        SstP = [[sb("S_0_%d_%d" % (s_, l), [128, 2, 64]) for l in range(NL)] for s_ in range(2)]
        SstS = [sb("S_1_%d" % l, [128, 2, 64]) for l in range(NL)]
        identf = t1[:, 0:128]
        smask = gB[:, 0:128]
        wsf = res[:, 0:512].rearrange("p (g j) -> p g j", g=4)
        wsb = mix[:, 0:512].rearrange("p (g j) -> p g j", g=4)
        mixT = xT
        qTf = thfq[:, 0:2, :]
        kTf = thfq[:, 2:4, :]
        tc_ = thfq[:].rearrange("p a b -> p (a b)")
        sq = ta
        ckf = t1[:, 0:128]
        cvf = t1[:, 128:256]
        ckb = kdup

        nunits = [0]
        uops = [0]
        KOPS = int(os.environ.get("KOPS", "100000000"))

        grp = [None]
        groups = {}

        def add(eng, fn, r=(), w=(), chan=None, rows=None):
            if grp[0] is None:
                return S.add(eng, fn, r=r, w=w, chan=chan, rows=rows)
            groups.setdefault(grp[0], []).append((eng, fn, tuple(r), tuple(w), chan, rows))

        ORDER = os.environ.get("KORDER", "in,gates,sguln,q,kv,fq_pe,attn_a,hg_dve,sgumix,hgrn_pe,attn_b,out,out_ln").split(",")
        DEFER = os.environ.get("KDEFER", "1") != "0"
        DEFER2 = os.environ.get("KDEFER2", "1") == "1"
        HG_FIRST = os.environ.get("KHGFIRST", "0") == "1"
        PREFETCH = os.environ.get("KPREF", "1") == "1"
        cur_pref = [None]
        pending_out = []
        pending_pref = []
        pending_ln_late = []
        LN_LATE = os.environ.get("KLNLATE", "1") == "1"
        pending_ln = []

        def emit_ops(lst):
            for (eng, fn, r, w, chan, rows) in lst:
                S.add(eng, fn, r=r, w=w, chan=chan, rows=rows)

        def drain_pending():
            emit_ops(pending_out)
            del pending_out[:]
            emit_ops(pending_pref)
            del pending_pref[:]
            emit_ops(pending_ln)
            del pending_ln[:]
            emit_ops(pending_ln_late)
            del pending_ln_late[:]

        def flush(defer, lyr=0):
            assert set(groups.keys()) <= set(ORDER), groups.keys()
            head = ("in", "gates", "q", "kv", "fq_pe", "sguln")
            if HG_FIRST:
                mid1 = ("hg_dve",)
                mid2 = ("attn_a",)
            else:
                mid1 = ()
                mid2 = ("attn_a", "hg_dve")
            rest = ("sgumix", "hgrn_pe", "attn_b")
            for g in head + mid1:
                emit_ops(groups.get(g, []))
            emit_ops(pending_out)
            del pending_out[:]
            if pending_pref:
                emit_ops(pending_pref)
                del pending_pref[:]
            emit_ops(pending_ln)
            del pending_ln[:]
            for g in mid2 + rest:
                emit_ops(groups.get(g, []))
            emit_ops(pending_ln_late)
            del pending_ln_late[:]
            if defer and DEFER:
                if PREFETCH and cur_pref[0] is not None:
                    pending_pref.append(cur_pref[0])
                pending_out.extend(groups.get("out", []))
                if LN_LATE and lyr == NL - 1:
                    pending_ln_late.extend(groups.get("out_ln", []))
                else:
                    pending_ln.extend(groups.get("out_ln", []))
            else:
                emit_ops(groups.get("out", []))
                emit_ops(groups.get("out_ln", []))
            groups.clear()
            grp[0] = None

        def wload(l, kmax=8):
            src = w_in[l].rearrange("(kc p) n -> p kc n", p=128)
            for (s0, d0, w) in W_SEGS:
                for k0 in range(0, kmax, 2):
                    add("pool", lambda e, s0=s0, d0=d0, w=w, src=src, l=l, k0=k0: e.dma_start(out=Win[:, l, k0:k0 + 2, d0:d0 + w], in_=src[:, k0:k0 + 2, s0:s0 + w]),
                        w=["Win%d" % l], chan="w%d" % l)

        def woload(l):
            src = w_out[l].rearrange("(kc p) n -> p kc n", p=128)
            for h in range(2):
                for k0 in range(0, 8, 2):
                    add("pool", lambda e, h=h, src=src, l=l, k0=k0: e.dma_start(out=Wout[:, l, k0:k0 + 2, h * 512:(h + 1) * 512], in_=src[:, k0:k0 + 2, h * 512:(h + 1) * 512]),
                        w=["Wout%d" % l], chan="wo%d" % l)

        SKIP = os.environ.get("KSKIP", "")
        HLOAD = os.environ.get("KHLOAD", "1") == "1"

        def ld(dst, src, name):
            if "P" in SKIP and "bc" in name:
                return
            if "R" in SKIP and name in ("bsT", "lbt"):
                return
            add("sp", lambda e: e.dma_start(out=dst, in_=src), w=[name], chan="c_" + name)

        ld(mask2[:], mask2_d, "mask2")
        ld(smask, smask_d, "gB")
        ld(cosP[:], cosp, "cosP")
        ld(sinP[:], sinp, "sinP")
        ld(cosS[:], coss, "cosS")
        ld(sinS[:], sins, "sinS")
        for l in range(NL):
            ld(lnvg_bc[:, l, :], lnvg[l:l + 1, :].partition_broadcast(128), "lnvg_bc")
            ld(lnvb_bc[:, l, :], lnvb[l:l + 1, :].partition_broadcast(128), "lnvb_bc")
            ld(lng_bc[:, l, :], lng[l:l + 1, :].partition_broadcast(128), "lng_bc")
            ld(lnb_bc[:, l, :], lnb[l:l + 1, :].partition_broadcast(128), "lnb_bc")
            ld(nbg_bc[:, l, :], nbg[l:l + 1, :].partition_broadcast(128), "nbg_bc")
            ld(esink[:, l, :], sinks[l:l + 1, :].partition_broadcast(128), "esink")
        ld(bsT[:], b_s.rearrange("l g i -> i l g"), "bsT")
        ld(lbt[:], lbp.rearrange("l (h c) -> c l h", c=128), "lbt")
        add("act", lambda e: e.activation(out=esink[:], in_=esink[:], func=AF.Exp), r=["esink"], w=["esink"])
        add("pool", lambda e: e.memset(bco[:, 0, :], 0.5), w=["bco"])
        add("dve", lambda e: e.tensor_tensor(out=lbt[:, 0, :], in0=lbt[:, 1, :], in1=lbt[:, 0, :], op=ALU.subtract), r=["lbt"], w=["lbt"])
        add("act", lambda e: e.activation(out=lbt[:, 1, :], in_=lbt[:, 0, :], func=AF.Tanh, scale=0.5), r=["lbt"], w=["lbt"])
        add("dve", lambda e: e.tensor_scalar(out=bco[:, 1, :], in0=lbt[:, 1, :], scalar1=-0.25, scalar2=0.25, op0=ALU.mult, op1=ALU.add), r=["lbt", "bco"], w=["bco"])
        add("dve", lambda e: e.tensor_scalar(out=nbco[:], in0=bco[:], scalar1=-1.0, scalar2=None, op0=ALU.mult), r=["bco"], w=["nbco"])
        add("pool", lambda e: e.memset(mhalf[:], -0.5), w=["mhalf"])
        add("pool", lambda e: e.memset(ones_c[:], 1.0), w=["ones_c"])
        add("pool", lambda e: e.memset(m1seg[:], 0.0), w=["m1seg"])
        add("pool", lambda e: e.memset(m1seg[:, 0:1], 1.0), w=["m1seg"])
        add("pool", lambda e: e.memset(identf, 0.0), w=["t1"])
        add("pool", lambda e: e.affine_select(out=identf, in_=identf, pattern=[[-1, 128]], compare_op=ALU.not_equal, fill=1.0, base=0, channel_multiplier=1), r=["t1"], w=["t1"])
        add("dve", lambda e: e.tensor_copy(out=ident[:], in_=identf), r=["t1"], w=["ident"])
        add("pool", lambda e: e.memset(pT[:], 0.0), w=["pT"])
        for i in range(NKV * NL):
            add("pool", lambda e, i=i: e.memset(vaug[i][:], 1.0), w=["vaug%d" % i])
        for l in range(NL if "S" not in SKIP else 0):
            add("sp", lambda e, l=l: e.dma_start(out=wsf, in_=w_s[l].rearrange("g i j -> i g j")), w=["res0", "res1"], chan="c_wsf")
            add("dve", lambda e: e.tensor_tensor(out=wsb, in0=wsf, in1=smask.unsqueeze(1).to_broadcast([128, 4, 128]), op=ALU.mult), r=["res0", "res1", "gB"], w=["mixA", "mixB"])
            for g in range(4):
                add("pe", lambda e, g=g: e.transpose(out=bkb(2)[:, g * 128:(g + 1) * 128], in_=wsb[:, g, :], identity=ident[:]), r=["mixA", "mixB", "ident"], w=["B2"])
            add("act", lambda e, l=l: e.activation(out=wsT[:, l, :, :], in_=bkb(2)[:, 0:512].rearrange("p (g i) -> p g i", g=4), func=AF.Identity), r=["B2"], w=["wsT"])

        if not HLOAD:
            wload(0)
            woload(0)
            wload(1)
            woload(1)
        else:
            wload(0, 6)
            stg = [(res[:, 0:512], "res0"), (res[:, 512:1024], "res1"), (thg[:, 0:512], "thg0"), (thg[:, 512:1024], "thg1")]
            jobs = []
            for kc in range(6, 8):
                for (s0, d0, w) in W_SEGS:
                    jobs.append((w_in[0][kc * 128:(kc + 1) * 128, s0:s0 + w], Win[:, 0, kc, d0:d0 + w], w, "Win0"))
            for kc in range(8):
                for h in range(2):
                    jobs.append((w_out[0][kc * 128:(kc + 1) * 128, h * 512:(h + 1) * 512], Wout[:, 0, kc, h * 512:(h + 1) * 512], 512, "Wout0"))
            for kc in range(8):
                for (s0, d0, w) in W_SEGS:
                    jobs.append((w_in[1][kc * 128:(kc + 1) * 128, s0:s0 + w], Win[:, 1, kc, d0:d0 + w], w, "Win1"))
            for kc in range(8):
                for h in range(2):
                    jobs.append((w_out[1][kc * 128:(kc + 1) * 128, h * 512:(h + 1) * 512], Wout[:, 1, kc, h * 512:(h + 1) * 512], 512, "Wout1"))
            for ji, (src, dst, w, rname) in enumerate(jobs):
                buf, bname = stg[ji % len(stg)]
                add("sp", lambda e, src=src, buf=buf, w=w: e.dma_start(out=buf[:, 0:w], in_=src), w=[bname], chan="stg%d" % (ji % len(stg)))
                if ji % 2 == 0:
                    add("dve", lambda e, dst=dst, buf=buf, w=w: e.tensor_copy(out=dst, in_=buf[:, 0:w]), r=[bname], w=[rname])
                else:
                    add("act", lambda e, dst=dst, buf=buf, w=w: e.copy(out=dst, in_=buf[:, 0:w]), r=[bname], w=[rname])

        def layer_tile(U):
            grp[0] = "in"
            P = U["P"]
            l = U["l"]
            kind = U["kind"]
            s = U["s"]
            kd = 0 if kind == "p" else 1
            xslot = U["xslot"]
            nch = 2 if P == 128 else 1
            Sname = ("S_0_%d_%d" % (s, l)) if kd == 0 else ("S_1_%d" % l)
            St = SstP[s][l] if kd == 0 else SstS[l]
            cosT, sinT = U["cos"], U["sin"]
            cosn, sinn = U["cosn"], U["sinn"]
            xin = xf[xslot]
            xinn = "xf%d" % xslot
            xbt = xb[xslot]
            xbn = "xb%d" % xslot
            Wl = "Win%d" % l

            if l == 0:
                if not U.get("prefetched", False):
                    add("sp", lambda e: e.dma_start(out=xin[:P, :], in_=U["xsrc"]), w=[xinn], chan="x%d" % xslot)
                add("act", lambda e: e.copy(out=xbt[:P, :], in_=xin[:P, :]), r=[xinn], w=[xbn])
            for kc in range(8):
                add("pe", lambda e, kc=kc: e.transpose(out=bkb(2)[:, kc * 128:kc * 128 + P], in_=xbt[:P, kc * 128:(kc + 1) * 128], identity=ident[:P, :P]),
                    r=[xbn, "ident"], w=["B2"], rows=(0, P))
            add("act", lambda e: e.activation(out=xT[:, :, :P], in_=bkb(2).rearrange("p (a b) -> p a b", a=8)[:, :, :P], func=AF.Identity), r=["B2"], w=["xT"])
            grp[0] = "gates"


            def inproj(bank, col0):
                for kc in range(8):
                    add("pe", lambda e, kc=kc: e.matmul(bkf(bank)[:P, :], lhsT=xT[:, kc, :P], rhs=Win[:, l, kc, col0:col0 + 512], start=(kc == 0), stop=(kc == 7)),
                        r=["xT", Wl], w=["B%d" % bank])

            inproj(0, 0)
            add("act", lambda e: e.activation(out=thg[:P, 0:512], in_=bkf(0)[:P, :], func=AF.Tanh, scale=0.5), r=["B0"], w=["thg0"])
            add("dve", lambda e: e.scalar_tensor_tensor(out=thg[:P, 0:512], in0=thg[:P, 0:512], scalar=1.0, in1=bkf(0)[:P, :], op0=ALU.add, op1=ALU.mult), r=["thg0", "B0"], w=["thg0"])
            inproj(1, 512)
            add("act", lambda e: e.activation(out=thg[:P, 512:1024], in_=bkf(1)[:P, :], func=AF.Tanh, scale=0.5), r=["B1"], w=["thg1"])
            add("dve", lambda e: e.scalar_tensor_tensor(out=thg[:P, 512:1024], in0=thg[:P, 512:1024], scalar=1.0, in1=bkf(1)[:P, :], op0=ALU.add, op1=ALU.mult), r=["thg1", "B1"], w=["thg1"])
            grp[0] = "sguln"
            inproj(3, O_U)
            add("dve", lambda e: e.tensor_copy(out=uf[:P, :], in_=bkf(3)[:P, 0:256]), r=["B3"], w=["uf"])
            add("dve", lambda e: e.bn_stats(out=st6[:P, 0, :], in_=bkf(3)[:P, 256:512]), r=["B3"], w=["st6"])
            add("dve", lambda e: e.bn_aggr(out=mv[:P, :], in_=st6[:P, 0, :]), r=["st6"], w=["mv"])
            add("dve", lambda e: e.tensor_scalar(out=veps[:P, :], in0=mv[:P, 1:2], scalar1=LN_EPS, scalar2=None, op0=ALU.add), r=["mv"], w=["veps"])
            add("pool", lambda e: e.tensor_tensor(out=rstd[:P, :], in0=veps[:P, :], in1=mhalf[:P, 0:1], op=ALU.pow), r=["veps", "mhalf"], w=["rstd"])
            add("dve", lambda e: e.tensor_scalar(out=vn0[:P, :], in0=bkf(3)[:P, 256:512], scalar1=mv[:P, 0:1], scalar2=rstd[:P, 0:1], op0=ALU.subtract, op1=ALU.mult), r=["B3", "mv", "rstd"], w=["vn0"])
            add("dve", lambda e: e.tensor_tensor(out=vn0[:P, :], in0=vn0[:P, :], in1=lnvg_bc[:P, l, :], op=ALU.mult), r=["vn0", "lnvg_bc"], w=["vn0"])
            if kind == "p":
                add("dve", lambda e: e.tensor_tensor(out=vnb[:P, :], in0=vn0[:P, :], in1=lnvb_bc[:P, l, :], op=ALU.add), r=["vn0", "lnvb_bc"], w=["vnb"])
            else:
                add("dve", lambda e: e.tensor_tensor(out=vn0[:P, :], in0=vn0[:P, :], in1=lnvb_bc[:P, l, :], op=ALU.add), r=["vn0", "lnvb_bc"], w=["vn0"])
                add("act", lambda e: e.copy(out=vnb[:P, :], in_=vn0[:P, :]), r=["vn0"], w=["vnb"])
            if kind == "s":
                i = add("sp", lambda e: e.dma_start(out=nva[l, s], in_=vn0[:P, :]), r=["vn0"], w=["o%d" % uid[0]], chan="o_va")
                finals.append("o%d" % uid[0]); uid[0] += 1
            grp[0] = "q"
            inproj(4, O_QC)
            add("act", lambda e: e.activation(out=qb[:P, :], in_=bkf(4)[:P, :], func=AF.Identity), r=["B4"], w=["qb"])

            def rope(ps_view, out_view, nh, rname, wname):
                cb = cosT.unsqueeze(1).to_broadcast([P, nh, 8])
                sbb = sinT.unsqueeze(1).to_broadcast([P, nh, 8])
                x1 = ps_view[:, :, 0:8]
                x2 = ps_view[:, :, 8:16]
                add("dve", lambda e: e.tensor_tensor(out=rt[:P, 0, :nh, :], in0=x1, in1=cb, op=ALU.mult), r=[rname, cosn], w=["rt0"])
                add("dve", lambda e: e.tensor_tensor(out=rt[:P, 1, :nh, :], in0=x2, in1=sbb, op=ALU.mult), r=[rname, sinn], w=["rt1"])
                add("dve", lambda e: e.tensor_tensor(out=rt[:P, 2, :nh, :], in0=x2, in1=cb, op=ALU.mult), r=[rname, cosn], w=["rt2"])
                add("dve", lambda e: e.tensor_tensor(out=rt[:P, 3, :nh, :], in0=x1, in1=sbb, op=ALU.mult), r=[rname, sinn], w=["rt3"])
                add("dve", lambda e: e.tensor_tensor(out=out_view[:, :, 0:8], in0=rt[:P, 0, :nh, :], in1=rt[:P, 1, :nh, :], op=ALU.subtract), r=["rt0", "rt1"], w=[wname])
                add("dve", lambda e: e.tensor_tensor(out=out_view[:, :, 8:16], in0=rt[:P, 2, :nh, :], in1=rt[:P, 3, :nh, :], op=ALU.add), r=["rt2", "rt3"], w=[wname])

            rope(bkf(4)[:P, :].rearrange("p (h d) -> p h d", h=8), qb[:P, :].rearrange("p (h d) -> p h d", h=8), 8, "B4", "qb")
            grp[0] = "kv"
            inproj(5, O_KC)
            add("act", lambda e: e.activation(out=kf[:P, :], in_=bkf(5)[:P, 0:128], func=AF.Identity), r=["B5"], w=["kf"])
            rope(bkf(5)[:P, 0:128].rearrange("p (h d) -> p h d", h=2), kf[:P, :].rearrange("p (h d) -> p h d", h=2), 2, "B5", "kf")
            kvs = U["kvslot"]
            for dup in range(2):
                add("act", lambda e, dup=dup: e.copy(out=kdup[:P, :, dup, :], in_=kf[:P, :].rearrange("p (g d) -> p g d", g=2)), r=["kf"], w=["kdup"])
            add("act", lambda e: e.activation(out=vaug[kvs][:P, :, 0:64], in_=bkf(5)[:P, 128:256].rearrange("p (g d) -> p g d", g=2), func=AF.Identity), r=["B5"], w=["vaug%d" % kvs])
            add("act", lambda e: e.activation(out=vb[:P, :], in_=bkf(5)[:P, 256:512], func=AF.Identity), r=["B5"], w=["vb"])
            if U["kvout"] is not None:
                ko, vo = U["kvout"]
                add("act", lambda e: e.activation(out=vf[:P, :], in_=bkf(5)[:P, 128:256], func=AF.Identity), r=["B5"], w=["vf"])
                add("sp", lambda e: e.dma_start(out=ko, in_=kf[:P, :]), r=["kf"], w=["o%d" % uid[0]], chan="o_k")
                finals.append("o%d" % uid[0]); uid[0] += 1
                add("sp", lambda e: e.dma_start(out=vo, in_=vf[:P, :]), r=["vf"], w=["o%d" % uid[0]], chan="o_v")
                finals.append("o%d" % uid[0]); uid[0] += 1

            grp[0] = "fq_pe"
            fq = bkf(6).rearrange("p (a b) -> p a b", a=4)
            for j in range(4):
                col0 = (O_QB + 128 * j) if j < 2 else (O_FB + 128 * (j - 2))
                for kc in range(8):
                    add("pe", lambda e, kc=kc, j=j, col0=col0: e.matmul(fq[:, j, :P], lhsT=Win[:, l, kc, col0:col0 + 128], rhs=xT[:, kc, :P], start=(kc == 0), stop=(kc == 7)),
                        r=["xT", Wl], w=["B6"])
            add("act", lambda e: e.activation(out=thfq[:, :, :P], in_=fq[:, :, :P], func=AF.Tanh, scale=0.5), r=["B6"], w=["thfq_q", "thfq_k"])
            add("dve", lambda e: e.scalar_tensor_tensor(out=qTf[:, :, :P], in0=thfq[:, 0:2, :P], scalar=1.0, in1=fq[:, 0:2, :P], op0=ALU.add, op1=ALU.mult), r=["thfq_q", "B6"], w=["thfq_q"])
            grp[0] = "hg_dve"
            for h in range(2):
                add("dve", lambda e, h=h: e.tensor_scalar(out=kTf[:, h, :P], in0=thfq[:, 2 + h, :P], scalar1=nbco[:, l, h:h + 1], scalar2=bco[:, l, h:h + 1], op0=ALU.mult, op1=ALU.add),
                    r=["thfq_k", "bco", "nbco"], w=["thfq_k"])
            add("dve", lambda e: e.tensor_scalar(out=Rf[:, :, :P], in0=kTf[:, :, :P], scalar1=-1.0, scalar2=1.0, op0=ALU.mult, op1=ALU.add), r=["thfq_k"], w=["Rf"])
            if P == 128:
                rtf = rt[:].rearrange("p a b c -> p (a b c)")
                Rf8 = Rf[:].rearrange("p h (s j) -> p (h s) j", j=32)
                add("dve", lambda e: e.tensor_tensor(out=rtf.rearrange("p (s j) -> p s j", j=32), in0=Rf8, in1=m1seg[:, :].unsqueeze(1).to_broadcast([128, 8, 32]), op=ALU.mult),
                    r=["Rf", "m1seg"], w=["rt0", "rt1", "rt2", "rt3"])
                add("dve", lambda e: e.tensor_tensor(out=Rf[:].rearrange("p h t -> p (h t)"), in0=Rf[:].rearrange("p h t -> p (h t)"), in1=rtf, op=ALU.subtract), r=["Rf", "rt0", "rt1", "rt2", "rt3"], w=["Rf"])
                add("dve", lambda e: e.tensor_tensor_scan(out=Pf[:].rearrange("p h t -> p (h t)"), data0=Rf[:].rearrange("p h t -> p (h t)"), data1=rtf, initial=0.0, op0=ALU.mult, op1=ALU.add),
                    r=["Rf", "rt0", "rt1", "rt2", "rt3"], w=["Pf"])
                add("dve", lambda e: e.reciprocal(out=Rf[:, :, :], in_=Pf[:, :, :]), r=["Pf"], w=["Rf"])
                Pf4 = Pf[:].rearrange("p h (c j) -> p h c j", c=2)
                Rf4 = Rf[:].rearrange("p h (c j) -> p h c j", c=2)
                add("dve", lambda e: e.tensor_copy(out=scl[:, :, :, 0], in_=Pf4[:, :, :, 31]), r=["Pf"], w=["scl"])
                add("dve", lambda e: e.tensor_copy(out=scl[:, :, :, 1], in_=Rf4[:, :, :, 31]), r=["Rf"], w=["scl"])
                add("dve", lambda e: e.tensor_copy(out=scl[:, :, :, 3], in_=Pf4[:, :, :, 63]), r=["Pf"], w=["scl"])
                add("dve", lambda e: e.tensor_tensor(out=scl[:, :, :, 2], in0=scl[:, :, :, 0], in1=scl[:, :, :, 3], op=ALU.mult), r=["scl"], w=["scl"])
                add("dve", lambda e: e.tensor_tensor(out=Pf4[:, :, :, 0:32], in0=Pf4[:, :, :, 0:32], in1=scl[:, :, :, 1:2].to_broadcast([128, 2, 2, 32]), op=ALU.mult), r=["Pf", "scl"], w=["Pf"])
                add("dve", lambda e: e.tensor_tensor(out=Rf4[:, :, :, 0:32], in0=Rf4[:, :, :, 0:32], in1=scl[:, :, :, 0:1].to_broadcast([128, 2, 2, 32]), op=ALU.mult), r=["Rf", "scl"], w=["Rf"])
            else:
                segs = [(0, 16)]
                lastH1 = [15]
                lastH2 = [None]
                for h in range(2):
                    for (a, b) in segs:
                        add("dve", lambda e, h=h, a=a, b=b: e.tensor_tensor_scan(out=Pf[:, h, a:b], data0=Rf[:, h, a:b], data1=ones_c[:, a:b], initial=1.0, op0=ALU.mult, op1=ALU.mult),
                            r=["Rf", "ones_c"], w=["Pf"])
                add("dve", lambda e: e.reciprocal(out=Rf[:, :, :P], in_=Pf[:, :, :P]), r=["Pf"], w=["Rf"])
                for c in range(nch):
                    i1 = lastH1[c]
                    add("dve", lambda e, c=c, i1=i1: e.tensor_copy(out=scl[:, :, c, 0], in_=Pf[:, :, i1]), r=["Pf"], w=["scl"])
                    add("dve", lambda e, c=c, i1=i1: e.tensor_copy(out=scl[:, :, c, 1], in_=Rf[:, :, i1]), r=["Rf"], w=["scl"])
                    add("dve", lambda e, c=c: e.memset(scl[:, :, c, 3], 1.0), w=["scl"])
                    add("dve", lambda e, c=c: e.tensor_tensor(out=scl[:, :, c, 2], in0=scl[:, :, c, 0], in1=scl[:, :, c, 3], op=ALU.mult), r=["scl"], w=["scl"])
                for c in range(nch):
                    a = 64 * c
                    b = lastH1[c] + 1
                    for h in range(2):
                        add("dve", lambda e, c=c, h=h, a=a, b=b: e.tensor_scalar(out=Pf[:, h, a:b], in0=Pf[:, h, a:b], scalar1=scl[:, h, c, 1:2], scalar2=None, op0=ALU.mult), r=["Pf", "scl"], w=["Pf"])
                        add("dve", lambda e, c=c, h=h, a=a, b=b: e.tensor_scalar(out=Rf[:, h, a:b], in0=Rf[:, h, a:b], scalar1=scl[:, h, c, 0:1], scalar2=None, op0=ALU.mult), r=["Rf", "scl"], w=["Rf"])
            add("dve", lambda e: e.tensor_tensor(out=qt[:, :, :P], in0=qTf[:, :, :P], in1=Pf[:, :, :P], op=ALU.mult), r=["thfq_q", "Pf"], w=["qt"])
            add("dve", lambda e: e.tensor_tensor(out=kt[:, :, :P], in0=kTf[:, :, :P], in1=Rf[:, :, :P], op=ALU.mult), r=["thfq_k", "Rf"], w=["kt"])

            grp[0] = "sgumix"
            mixed = bkf(7)[:, 256:512]
            for g in range(4):
                add("pe", lambda e, g=g: e.matmul(mixed[:P, g * 64:(g + 1) * 64], lhsT=wsT[:P, l, g, :P], rhs=vnb[:P, g * 64:(g + 1) * 64], start=True, stop=True),
                    r=["wsT", "vnb"], w=["B7b"], rows=(0, P))
            add("dve", lambda e: e.tensor_tensor(out=ta[:P, :].rearrange("p (g d) -> p g d", g=4), in0=mixed[:P, :].rearrange("p (g d) -> p g d", g=4),
                                                 in1=bsT[:P, l, :].unsqueeze(2).to_broadcast([P, 4, 64]), op=ALU.add), r=["B7b", "bsT"], w=["ta"])
            add("dve", lambda e: e.tensor_tensor(out=ta[:P, :], in0=ta[:P, :], in1=uf[:P, :], op=ALU.mult), r=["ta", "uf"], w=["ta"])
            add("dve", lambda e: e.tensor_tensor(out=mix[:P, 0:256], in0=ta[:P, :], in1=thg[:P, 0:256], op=ALU.mult), r=["ta", "thg0"], w=["mixA"])

            grp[0] = "hgrn_pe"
            Aps = bkf(7)[:, 0:256].rearrange("p (h i) -> p h i", h=4)
            cl = min(64, P)
            for hh in range(2):
                for c in range(nch):
                    a = 64 * c
                    for hch in range(2):
                        h = 2 * hch + hh
                        add("pe", lambda e, a=a, cl=cl, h=h, hch=hch, hh=hh: e.matmul(Aps[a:a + cl, h, 0:cl], lhsT=kt[64 * hh:64 * hh + 64, hch, a:a + cl], rhs=qt[64 * hh:64 * hh + 64, hch, a:a + cl], start=True, stop=True),
                            r=["kt", "qt"], w=["B7a"], rows=(64 * hh, 64 * hh + 64))
            add("dve", lambda e: e.tensor_tensor(out=At[:P, :, 0:cl], in0=Aps[:P, :, 0:cl], in1=mask2[:P, 0:cl].unsqueeze(1).to_broadcast([P, 4, cl]), op=ALU.mult), r=["B7a", "mask2"], w=["At"])
            for h in range(2):
                add("pe", lambda e, h=h: e.transpose(out=bkb(2)[:P, h * 128:(h + 1) * 128], in_=kt[:, h, :P], identity=ident[:]), r=["kt", "ident"], w=["B2"])
            add("act", lambda e: e.activation(out=ktok[:P, :], in_=bkb(2)[:P, 0:256], func=AF.Identity), r=["B2"], w=["ktok"])
            ohg = bkf(7)[:, 256:512]
            kvps = [bkf(3 + c)[:, 288:416].rearrange("p (a b) -> p a b", a=2) for c in range(2)]
            for c in range(nch):
                a = 64 * c
                for h in range(4):
                    hch, hh = h // 2, h % 2
                    add("pe", lambda e, a=a, cl=cl, h=h, hch=hch, hh=hh, c=c: e.matmul(kvps[c][64 * hh:64 * hh + 64, hch, :], lhsT=ktok[a:a + cl, h * 64:(h + 1) * 64], rhs=vb[a:a + cl, h * 64:(h + 1) * 64], start=True, stop=True),
                        r=["ktok", "vb"], w=["B%db" % (3 + c)], rows=(a, a + cl))
            for c in range(nch):
                add("dve", lambda e, c=c: e.tensor_tensor(out=Sp[:, c, :, :], in0=St[:, :, :], in1=scl[:, :, c, 0:1].to_broadcast([128, 2, 64]), op=ALU.mult), r=[Sname, "scl"], w=["Sp%d" % c])
                add("dve", lambda e, c=c: e.tensor_tensor(out=kvt[:, :, :], in0=kvps[c][:, :, :], in1=scl[:, :, c, 3:4].to_broadcast([128, 2, 64]), op=ALU.mult), r=["B%db" % (3 + c), "scl"], w=["kvt"])
                add("dve", lambda e, c=c: e.tensor_tensor(out=St[:, :, :], in0=St[:, :, :], in1=scl[:, :, c, 2:3].to_broadcast([128, 2, 64]), op=ALU.mult), r=[Sname, "scl"], w=[Sname])
                add("dve", lambda e, c=c: e.tensor_tensor(out=St[:, :, :], in0=St[:, :, :], in1=kvt[:, :, :], op=ALU.add), r=[Sname, "kvt"], w=[Sname])
            for c in range(nch):
                a = 64 * c
                for h in range(4):
                    hch, hh = h // 2, h % 2
                    add("pe", lambda e, a=a, cl=cl, h=h: e.matmul(ohg[a:a + cl, h * 64:(h + 1) * 64], lhsT=At[a:a + cl, h, 0:cl], rhs=vb[a:a + cl, h * 64:(h + 1) * 64], start=True, stop=False),
                        r=["At", "vb"], w=["B7b"], rows=(a, a + cl))
                    add("pe", lambda e, a=a, cl=cl, h=h, hch=hch, hh=hh, c=c: e.matmul(ohg[a:a + cl, h * 64:(h + 1) * 64], lhsT=qt[64 * hh:64 * hh + 64, hch, a:a + cl], rhs=Sp[64 * hh:64 * hh + 64, c, hch, :], start=False, stop=True),
                        r=["qt", "Sp%d" % c], w=["B7b"], rows=(64 * hh, 64 * hh + 64))
            if U["Sout"] is not None:
                add("sp", lambda e: e.dma_start(out=U["Sout"].rearrange("(a b) k v -> (b k) a v", a=2), in_=St[:]), r=[Sname], w=["o%d" % uid[0]], chan="o_S")
                finals.append("o%d" % uid[0]); uid[0] += 1
            add("act", lambda e: e.activation(out=sq[:P, :], in_=ohg[:P, :], func=AF.Square), r=["B7b"], w=["ta"])
            add("dve", lambda e: e.tensor_reduce(out=ss[:P, :], in_=sq[:P, :].rearrange("p (h v) -> p h v", h=4), axis=AX, op=ALU.add), r=["ta"], w=["ss"])
            add("dve", lambda e: e.tensor_scalar(out=ss2[:P, :], in0=ss[:P, :], scalar1=1.0 / 64.0, scalar2=4.0 * RMS_EPS, op0=ALU.mult, op1=ALU.add), r=["ss"], w=["ss2"])
            add("pool", lambda e: e.tensor_tensor(out=rs[:P, :], in0=ss2[:P, :], in1=mhalf[:P, :], op=ALU.pow), r=["ss2", "mhalf"], w=["rs"])
            add("dve", lambda e: e.tensor_tensor(out=t1[:P, :].rearrange("p (h v) -> p h v", h=4), in0=ohg[:P, :].rearrange("p (h v) -> p h v", h=4), in1=rs[:P, :].unsqueeze(2).to_broadcast([P, 4, 64]), op=ALU.mult), r=["B7b", "rs"], w=["t1"])
            add("dve", lambda e: e.tensor_tensor(out=gB[:P, :].rearrange("p (h v) -> p h v", h=4), in0=thg[:P, 256:512].rearrange("p (h v) -> p h v", h=4), in1=nbg_bc[:P, l, :].unsqueeze(1).to_broadcast([P, 4, 64]), op=ALU.mult), r=["thg0", "nbg_bc"], w=["gB"])
            add("dve", lambda e: e.tensor_tensor(out=mix[:P, 256:512], in0=t1[:P, :], in1=gB[:P, :], op=ALU.mult), r=["t1", "gB"], w=["mixB"])

            grp[0] = "attn_a"
            for blk in range(4):
                add("pe", lambda e, blk=blk: e.transpose(out=bkb(2)[:, blk * 128:blk * 128 + P], in_=qb[:P, blk * 128:(blk + 1) * 128], identity=ident[:P, :P]), r=["qb", "ident"], w=["B2"], rows=(0, P))
            for g in range(2):
                add("pe", lambda e, g=g: e.transpose(out=bkb(2)[:, 512 + g * 128:512 + g * 128 + P], in_=kdup[:P, g, :, :].rearrange("p a d -> p (a d)"), identity=ident[:P, :P]), r=["kdup", "ident"], w=["B2"], rows=(0, P))
            add("act", lambda e: e.activation(out=qT[:, :, :P], in_=bkb(2)[:, 0:512].rearrange("p (a b) -> p a b", a=4)[:, :, :P], func=AF.Identity), r=["B2"], w=["qT"])
            add("act", lambda e: e.activation(out=kT2[kvs][:, :, :P], in_=bkb(2)[:, 512:768].rearrange("p (a b) -> p a b", a=2)[:, :, :P], func=AF.Identity), r=["B2"], w=["kT2_%d" % kvs])
            ktl = []
            if U["prev"] is not None:
                ktl.append((U["prev"], 128, 0))
            ktl.append((kvs, P, 1))
            sctr = 0
            pT5 = pT[:].rearrange("p k (j t) q -> p k j t q", t=2)
            for (slot, nk, kti) in ktl:
                for hh in range(2):
                    sbank = sctr % 2
                    sctr += 1
                    sps = bkf(sbank).rearrange("p (h q) -> p h q", h=4)
                    sname = "B%d" % sbank
                    for jh in range(4):
                        g = jh // 2
                        add("pe", lambda e, slot=slot, nk=nk, jh=jh, hh=hh, g=g, sps=sps: e.matmul(sps[:nk, jh, :P], lhsT=kT2[slot][64 * hh:64 * hh + 64, g, :nk], rhs=qT[64 * hh:64 * hh + 64, jh, :P], start=True, stop=True),
                            r=["kT2_%d" % slot, "qT"], w=[sname], rows=(64 * hh, 64 * hh + 64))
                    if kind == "p":
                        if kti == 0:
                            regs = [(0, 128, 0, 64), (64, 128, 64, 128)]
                        else:
                            regs = [(0, 64, 0, 64), (0, 128, 64, 128)]
                    else:
                        regs = [(0, nk, 0, P)]
                    for (k0, k1, q0, q1) in regs:
                        add("act", lambda e, kti=kti, hh=hh, k0=k0, k1=k1, q0=q0, q1=q1, sps=sps: e.activation(out=pT5[k0:k1, kti, :, hh, q0:q1], in_=sps[k0:k1, :, q0:q1], func=AF.Exp, scale=0.125),
                            r=[sname], w=["pT"])
            oat = [bkf(3)[:, 0:260].rearrange("p (h d) -> p h d", h=4), bkf(4)[:, 0:260].rearrange("p (h d) -> p h d", h=4)]
            for h in range(8):
                hg, hl = h // 4, h % 4
                g = h // 4
                for i, (slot, nk, kti) in enumerate(ktl):
                    add("pe", lambda e, slot=slot, nk=nk, kti=kti, h=h, hg=hg, hl=hl, g=g, i=i: e.matmul(oat[hg][:P, hl, :], lhsT=pT[:nk, kti, h, :P], rhs=vaug[slot][:nk, g, :], start=(i == 0), stop=(i == len(ktl) - 1)),
                        r=["pT", "vaug%d" % slot], w=["B%da" % (3 + hg)], rows=(0, nk))
            grp[0] = "attn_b"
            for hg in range(2):
                bn = "B%da" % (3 + hg)
                add("dve", lambda e, hg=hg: e.tensor_tensor(out=den[:P, hg * 4:hg * 4 + 4], in0=oat[hg][:P, :, 64], in1=esink[:P, l, hg * 4:hg * 4 + 4], op=ALU.add), r=[bn, "esink"], w=["den%d" % hg])
                add("dve", lambda e, hg=hg: e.reciprocal(out=rden[:P, hg * 4:hg * 4 + 4], in_=den[:P, hg * 4:hg * 4 + 4]), r=["den%d" % hg], w=["rden%d" % hg])
                add("dve", lambda e, hg=hg: e.tensor_tensor(out=tc_[:P, hg * 256:(hg + 1) * 256].rearrange("p (h d) -> p h d", h=4), in0=oat[hg][:P, :, 0:64], in1=rden[:P, hg * 4:hg * 4 + 4].unsqueeze(2).to_broadcast([P, 4, 64]), op=ALU.mult),
                    r=[bn, "rden%d" % hg], w=["thfq_q" if hg == 0 else "thfq_k"])
                add("dve", lambda e, hg=hg: e.tensor_tensor(out=mix[:P, 512 + hg * 256:512 + (hg + 1) * 256], in0=tc_[:P, hg * 256:(hg + 1) * 256], in1=thg[:P, 512 + hg * 256:512 + (hg + 1) * 256], op=ALU.mult),
                    r=["thfq_q" if hg == 0 else "thfq_k", "thg1"], w=["mixC%d" % hg])

            grp[0] = "out"
            for kc in range(8):
                add("pe", lambda e, kc=kc: e.transpose(out=bkb(2)[:, kc * 128:kc * 128 + P], in_=mix[:P, kc * 128:(kc + 1) * 128], identity=ident[:P, :P]),
                    r=["mixA", "mixB", "mixC0", "mixC1", "ident"], w=["B2"], rows=(0, P))
            add("act", lambda e: e.activation(out=mixT[:, :, :P], in_=bkb(2).rearrange("p (a b) -> p a b", a=8)[:, :, :P], func=AF.Identity), r=["B2"], w=["xT"])
            for nb in range(2):
                for kc in range(8):
                    add("pe", lambda e, kc=kc, nb=nb: e.matmul(bkf(nb)[:P, :], lhsT=mixT[:, kc, :P], rhs=Wout[:, l, kc, nb * 512:(nb + 1) * 512], start=(kc == 0), stop=(kc == 7)),
                        r=["xT", "Wout%d" % l], w=["B%d" % nb])
                add("dve", lambda e, nb=nb: e.scalar_tensor_tensor(out=res[:P, nb * 512:(nb + 1) * 512], in0=bkf(nb)[:P, :], scalar=0.5 / ALPHA, in1=xin[:P, nb * 512:(nb + 1) * 512], op0=ALU.mult, op1=ALU.add),
                    r=["B%d" % nb, xinn], w=["res%d" % nb])
                add("dve", lambda e, nb=nb: e.bn_stats(out=st12[:P, nb, :], in_=res[:P, nb * 512:(nb + 1) * 512]), r=["res%d" % nb], w=["st12_%d" % nb])
            if dbg is not None and kind == "p" and dbg["sel"] == (s, U["t"], l):
                add("sp", lambda e: e.dma_start(out=dbg["mix"], in_=mix[:, :]), r=["mixA", "mixB", "mixC0", "mixC1"], w=["odbg0"], chan="o_dbg")
                add("sp", lambda e: e.dma_start(out=dbg["res"], in_=res[:, :]), r=["res0", "res1"], w=["odbg1"], chan="o_dbg")
                add("sp", lambda e: e.dma_start(out=dbg["thg"], in_=thg[:, :]), r=["thg0", "thg1"], w=["odbg2"], chan="o_dbg")
                finals.extend(["odbg0", "odbg1", "odbg2"])
            grp[0] = "out_ln"
            add("dve", lambda e: e.bn_aggr(out=mv2[:P, :], in_=st12[:P, :, :].rearrange("p a b -> p (a b)")), r=["st12_0", "st12_1"], w=["mv2"])
            add("dve", lambda e: e.tensor_scalar(out=veps2[:P, :], in0=mv2[:P, 1:2], scalar1=LN_EPS / (ALPHA * ALPHA), scalar2=None, op0=ALU.add), r=["mv2"], w=["veps2"])
            add("pool", lambda e: e.tensor_tensor(out=rstd2[:P, :], in0=veps2[:P, :], in1=mhalf[:P, 0:1], op=ALU.pow), r=["veps2", "mhalf"], w=["rstd2"])
            add("dve", lambda e: e.scalar_tensor_tensor(out=nmr[:P, :], in0=mv2[:P, 0:1], scalar=-1.0, in1=rstd2[:P, :], op0=ALU.mult, op1=ALU.mult), r=["mv2", "rstd2"], w=["nmr"])
            add("dve", lambda e: e.tensor_scalar(out=res[:P, :], in0=res[:P, :], scalar1=rstd2[:P, 0:1], scalar2=nmr[:P, 0:1], op0=ALU.mult, op1=ALU.add), r=["res0", "res1", "nmr", "rstd2"], w=["res0", "res1"])
            add("dve", lambda e: e.tensor_tensor(out=res[:P, :], in0=res[:P, :], in1=lng_bc[:P, l, :], op=ALU.mult), r=["res0", "res1", "lng_bc"], w=["res0", "res1"])
            if l == 0:
                add("dve", lambda e: e.tensor_tensor(out=xin[:P, :], in0=res[:P, :], in1=lnb_bc[:P, l, :], op=ALU.add), r=["res0", "res1", "lnb_bc"], w=[xinn])
                add("act", lambda e: e.copy(out=xbt[:P, :], in_=xin[:P, :]), r=[xinn], w=[xbn])
            else:
                add("dve", lambda e: e.tensor_tensor(out=res[:P, :], in0=res[:P, :], in1=lnb_bc[:P, l, :], op=ALU.add), r=["res0", "res1", "lnb_bc"], w=["res0", "res1"])
                add("sp", lambda e: e.dma_start(out=U["ydst"], in_=res[:P, :]), r=["res0", "res1"], w=["o%d" % uid[0]], chan="o_y")
                finals.append("o%d" % uid[0]); uid[0] += 1
            flush(kind == "p", l)

        kvctr = [0, 0]
        for s in range(2):
            for l in range(NL):
                add("pool", lambda e, s=s, l=l: e.memset(SstP[s][l][:], 0.0), w=["S_0_%d_%d" % (s, l)])
        prevs = [[None, None], [None, None]]
        for t in range(NT):
            for l in range(NL):
                for s in range(2):
                    slot = l * NKV + (kvctr[l] % NKV)
                    kvctr[l] += 1
                    U = dict(P=128, l=l, kind="p", s=s, xslot=s, t=t,
                             cos=cosP[:, t, :], sin=sinP[:, t, :], cosn="cosP", sinn="sinP",
                             xsrc=xp[s, t * 128:(t + 1) * 128, :],
                             kvslot=slot, prev=prevs[s][l],
                             kvout=(nkp[l, s], nvp[l, s]) if t == NT - 1 else None,
                             Sout=nhp[l, s] if t == NT - 1 else None,
                             ydst=yp[s, t * 128:(t + 1) * 128, :])
                    U["prefetched"] = PREFETCH and DEFER and l == 0 and t > 0
                    if l == 1 and t + 1 < NT:
                        cur_pref[0] = ("sp", (lambda e, s=s, t=t: e.dma_start(out=xf[s][:, :], in_=xp[s, (t + 1) * 128:(t + 2) * 128, :])),
                                       (), ("xf%d" % s,), "x%d" % s, None)
                    else:
                        cur_pref[0] = None
                    layer_tile(U)
                    prevs[s][l] = slot

        drain_pending()
        for s in range(2):
            xslot = s
            for l in range(NL):
                slotc = l * NKV + (kvctr[l] % NKV)
                kvctr[l] += 1
                slotn = l * NKV + (kvctr[l] % NKV)
                kvctr[l] += 1
                add("sp", lambda e, s=s, l=l: e.dma_start(out=ckf, in_=ck[l, s]), w=["t1"], chan="c_ckf")
                add("sp", lambda e, s=s, l=l: e.dma_start(out=cvf, in_=cv[l, s]), w=["t1"], chan="c_ckf")
                for dup in range(2):
                    add("act", lambda e, dup=dup: e.copy(out=ckb[:, :, dup, :], in_=ckf.rearrange("p (g d) -> p g d", g=2)), r=["t1"], w=["kdup"])
                for g in range(2):
                    add("pe", lambda e, g=g: e.transpose(out=bkb(2)[:, g * 128:(g + 1) * 128], in_=ckb[:, g, :, :].rearrange("p a d -> p (a d)"), identity=ident[:]), r=["kdup", "ident"], w=["B2"])
                add("act", lambda e, slotc=slotc: e.activation(out=kT2[slotc][:, :, :], in_=bkb(2)[:, 0:256].rearrange("p (a b) -> p a b", a=2), func=AF.Identity), r=["B2"], w=["kT2_%d" % slotc])
                add("act", lambda e, slotc=slotc: e.copy(out=vaug[slotc][:, :, 0:64], in_=cvf.rearrange("p (g d) -> p g d", g=2)), r=["t1"], w=["vaug%d" % slotc])
                add("sp", lambda e, s=s, l=l: e.dma_start(out=SstS[l][:], in_=st_in[l, s].rearrange("(a b) k v -> (b k) a v", a=2)), w=["S_1_%d" % l], chan="c_S%d" % l)
                cur_pref[0] = None
                U = dict(P=16, l=l, kind="s", s=s, xslot=xslot, t=0,
                         cos=cosS[:, :], sin=sinS[:, :], cosn="cosS", sinn="sinS",
                         xsrc=xs[s], kvslot=slotn, prev=slotc,
                         kvout=(nks[l, s], nvs[l, s]), Sout=nhs[l, s], ydst=ys[s])
                layer_tile(U)
        add("sp", None, r=finals)
        with nc.allow_non_contiguous_dma(reason="tiny param transposes"):
            S.emit(nc, stack)
    return nc, S


_CACHE = {}
_LAST = None


def _rope_tables(pos):
    half = 8
    inv = np.power(np.float32(500000.0), -np.arange(0, 16, 2, dtype=np.float32) / np.float32(16)).astype(np.float32)
    ang = pos.astype(np.float32)[:, None] * inv[None, :]
    return np.cos(ang).astype(np.float32), np.sin(ang).astype(np.float32)


def kernel(x_prompt, x_sample, cache_k, cache_v, state_hgrn, w_in, ln_v_g, ln_v_b, w_s, b_s,
           lb_param, norm_b_g, sinks, w_out, ln_g, ln_b):
    f = lambda a: np.ascontiguousarray(np.asarray(a, dtype=np.float32))
    x_prompt = f(x_prompt); x_sample = f(x_sample)
    B, T, _ = x_prompt.shape
    NT = T // 128
    if NT not in _CACHE:
        _CACHE[NT] = build(NT)
    nc, S = _CACHE[NT]
    cp, sp_ = _rope_tables(np.arange(T))
    cosp = np.ascontiguousarray(cp.reshape(NT, 128, 8).transpose(1, 0, 2))
    sinp = np.ascontiguousarray(sp_.reshape(NT, 128, 8).transpose(1, 0, 2))
    cs, ss_ = _rope_tables(PAST + np.arange(16))
    p = np.arange(128)
    mask2 = ((p[:, None] % 64) <= np.arange(64)[None, :]).astype(np.float32)
    smask = ((p[None, :] // 64) <= (p[:, None] // 64)).astype(np.float32)
    ck = f(cache_k).reshape(NL, B, 128, 128)
    cv = f(cache_v).reshape(NL, B, 128, 128)
    st = f(state_hgrn)
    shared = dict(w_in=f(w_in), w_out=f(w_out), lnvg=f(ln_v_g), lnvb=f(ln_v_b), w_s=f(w_s), b_s=f(b_s),
                  lbp=f(lb_param), nbg=f(norm_b_g), sinks=f(sinks), lng=f(ln_g), lnb=f(ln_b),
                  cosp=cosp, sinp=sinp, coss=cs, sins=ss_, mask2=mask2, smask=smask)
    in_maps = []
    for c in range(NCORES):
        m = dict(shared)
        m["xp"] = np.ascontiguousarray(x_prompt[2 * c:2 * c + 2])
        m["xs"] = np.ascontiguousarray(x_sample[2 * c:2 * c + 2])
        m["ck"] = np.ascontiguousarray(ck[:, 2 * c:2 * c + 2])
        m["cv"] = np.ascontiguousarray(cv[:, 2 * c:2 * c + 2])
        m["st"] = np.ascontiguousarray(st[:, 2 * c:2 * c + 2])
        in_maps.append(m)
    res = run_bass_kernel_spmd(nc, in_maps, core_ids=list(range(NCORES)))
    R = res.results
    global _LAST
    _LAST = R

    def cat(name, axis):
        return np.concatenate([np.asarray(r[name], dtype=np.float32) for r in R], axis=axis)

    y_p = cat("yp", 0)
    y_s = cat("ys", 0)
    nkp = cat("nkp", 1).reshape(NL, B, 128, 2, 64)
    nvp = cat("nvp", 1).reshape(NL, B, 128, 2, 64)
    nhp = cat("nhp", 1)
    nks = cat("nks", 1).reshape(NL, B, 16, 2, 64)
    nvs = cat("nvs", 1).reshape(NL, B, 16, 2, 64)
    nhs = cat("nhs", 1)
    nva = cat("nva", 1)
    return (y_p, y_s, nkp, nvp, nhp, nks, nvs, nhs, nva)
```

```python
import numpy as np
from contextlib import ExitStack
import concourse.bass as bass
import concourse.mybir as mybir
from concourse.bass_utils import run_bass_kernel_spmd

F32 = mybir.dt.float32
BF16 = mybir.dt.bfloat16
ALU = mybir.AluOpType
AF = mybir.ActivationFunctionType
AX = mybir.AxisListType.X

D = 1024
INW = 3072
NL = 2
ALPHA = float((2 * NL) ** 0.25)
LN_EPS = 1e-5
RMS_EPS = 1e-6
PAST = 2048
NCORES = 8
import os
MAX_UNITS = int(os.environ.get("KMAXU", "100000"))
KSTOP = int(os.environ.get("KSTOP", "1000"))

O_ZA, O_ZB, O_ZC, O_U, O_V, O_QC, O_KC, O_VC, O_IB, O_QB, O_FB = 0, 256, 512, 1024, 1280, 1536, 2048, 2176, 2304, 2560, 2816
W_SEGS = [(512, 0, 256), (1536, 256, 256), (2560, 512, 512), (0, 1024, 512), (1792, 1536, 512),
          (2304, 2048, 256), (1280, 2304, 256), (768, 2560, 512)]


class _Op:
    __slots__ = ("idx", "eng", "fn", "chan", "deps", "signal", "ev", "dma", "rows", "pe_self")

    def __init__(self, idx, eng, fn, chan, rows=None):
        self.idx = idx
        self.eng = eng
        self.fn = fn
        self.chan = chan
        self.dma = chan is not None
        self.deps = ()
        self.signal = False
        self.ev = None
        self.rows = rows
        self.pe_self = set()


class Sched:
    ENGS = ("pe", "act", "dve", "pool", "sp")

    def __init__(self):
        self.ops = []
        self.res = {}
        self.bank_last = {}
        self.last_dve = None
        self.pending_pool = None

    def add(self, eng, fn, r=(), w=(), chan=None, rows=None):
        idx = len(self.ops)
        op = _Op(idx, eng, fn, chan, rows)
        deps = set()
        bset = set()
        for R in list(r) + list(w):
            if len(R) >= 2 and R[0] == "B" and R[1].isdigit():
                bset.add(R[:2])
        for b in bset:
            lb = self.bank_last.setdefault(b, {})
            for e2, i2 in lb.items():
                if e2 != eng:
                    deps.add(i2)
                elif eng == "pe":
                    r1 = self.ops[i2].rows or (0, 128)
                    r2 = rows or (0, 128)
                    if r1[1] <= r2[0] or r2[1] <= r1[0]:
                        deps.add(i2)
                        op.pe_self.add(i2)
            lb[eng] = idx
        if eng == "pool" and chan is None and fn is not None:
            if self.last_dve is not None:
                deps.add(self.last_dve)
            self.pending_pool = idx
        if eng == "dve":
            if self.pending_pool is not None:
                deps.add(self.pending_pool)
                self.pending_pool = None
            self.last_dve = idx
        for R in r:
            st = self.res.get(R)
            if st is None:
                st = self.res[R] = [[], []]
            deps.update(st[0])
        for R in w:
            st = self.res.get(R)
            if st is None:
                st = self.res[R] = [[], []]
            deps.update(st[0])
            deps.update(st[1])
        for R in r:
            self.res[R][1].append(idx)
        for R in w:
            st = self.res[R]
            if st[1]:
                st[0] = [idx]
                st[1] = []
            else:
                st[0].append(idx)
        op.deps = deps
        self.ops.append(op)
        return idx

    def emit(self, nc, stack):
        ops = self.ops
        need = [[] for _ in ops]
        for op in ops:
            for d in sorted(op.deps):
                a = ops[d]
                if a.eng == "pe" and op.eng == "pe" and not a.dma and not op.dma and d not in op.pe_self:
                    continue
                a.signal = True
                need[op.idx].append(d)
        sems = {}
        for e in self.ENGS:
            sems[e] = stack.enter_context(nc.semaphore("s_" + e))
        chans = {}
        cnt = {e: 0 for e in self.ENGS}
        ccnt = {}
        for op in ops:
            if op.dma:
                if op.chan not in chans:
                    chans[op.chan] = stack.enter_context(nc.semaphore("c_" + str(op.chan)))
                    ccnt[op.chan] = 0
                ccnt[op.chan] += 16
                op.ev = (op.chan, ccnt[op.chan])
            elif op.signal and op.fn is not None:
                cnt[op.eng] += 1
                op.ev = (op.eng, cnt[op.eng])
        allsem = dict(sems)
        allsem.update(chans)
        per_eng = {e: [op for op in ops if op.eng == e] for e in self.ENGS}
        nwaits = {e: 0 for e in self.ENGS}

        def run(e, eng):
            seen = {}
            for op in per_eng[e]:
                for d in need[op.idx]:
                    a = ops[d]
                    if a.ev is None:
                        continue
                    s, v = a.ev
                    if seen.get(s, 0) >= v:
                        continue
                    seen[s] = v
                    eng.wait_ge(allsem[s], v)
                    nwaits[e] += 1
                if op.fn is None:
                    continue
                ins = op.fn(eng)
                if op.dma:
                    ins.then_inc(chans[op.chan], 16)
                elif op.signal:
                    ins.then_inc(sems[e], 1)

        block = stack.enter_context(nc.Block())

        @block.tensor
        def _(eng):
            run("pe", eng)

        @block.scalar
        def _(eng):
            run("act", eng)

        @block.vector
        def _(eng):
            run("dve", eng)

        @block.gpsimd
        def _(eng):
            run("pool", eng)

        @block.sync
        def _(eng):
            run("sp", eng)

        self.stats = dict(nops=len(ops), cnt=cnt, nwaits=nwaits, nchan=len(chans))


def build(NT):
    nc = bass.Bass("TRN2", target_bir_lowering=False)
    T = NT * 128

    def din(name, shape):
        return nc.dram_tensor(name, shape, F32, kind="ExternalInput").ap()

    def dout(name, shape):
        return nc.dram_tensor(name, shape, F32, kind="ExternalOutput").ap()

    xp = din("xp", [2, T, D])
    xs = din("xs", [2, 16, D])
    ck = din("ck", [NL, 2, 128, 128])
    cv = din("cv", [NL, 2, 128, 128])
    st_in = din("st", [NL, 2, 4, 64, 64])
    w_in = din("w_in", [NL, D, INW])
    w_out = din("w_out", [NL, D, D])
    lnvg = din("lnvg", [NL, 256])
    lnvb = din("lnvb", [NL, 256])
    w_s = din("w_s", [NL, 4, 128, 128])
    b_s = din("b_s", [NL, 4, 128])
    lbp = din("lbp", [NL, 256])
    nbg = din("nbg", [NL, 64])
    sinks = din("sinks", [NL, 8])
    lng = din("lng", [NL, D])
    lnb = din("lnb", [NL, D])
    cosp = din("cosp", [128, NT, 8])
    sinp = din("sinp", [128, NT, 8])
    coss = din("coss", [16, 8])
    sins = din("sins", [16, 8])
    mask2_d = din("mask2", [128, 64])
    smask_d = din("smask", [128, 128])

    yp = dout("yp", [2, T, D])
    ys = dout("ys", [2, 16, D])
    nkp = dout("nkp", [NL, 2, 128, 128])
    nvp = dout("nvp", [NL, 2, 128, 128])
    nhp = dout("nhp", [NL, 2, 4, 64, 64])
    nks = dout("nks", [NL, 2, 16, 128])
    nvs = dout("nvs", [NL, 2, 16, 128])
    nhs = dout("nhs", [NL, 2, 4, 64, 64])
    nva = dout("nva", [NL, 2, 16, 256])

    KDBG = os.environ.get("KDBG", "")
    dbg = None
    if KDBG:
        dbg = dict(mix=nc.dram_tensor("dbg_mix", [128, 1024], BF16, kind="ExternalOutput").ap(),
                   res=dout("dbg_res", [128, 1024]), thg=dout("dbg_thg", [128, 1024]),
                   sel=tuple(int(v) for v in KDBG.split(",")))
    S = Sched()
    finals = []
    uid = [0]

    with ExitStack() as stack:
        def sb(name, shape, dt=F32):
            return stack.enter_context(nc.sbuf_tensor("sb_" + name, shape, dt))

        banks = [stack.enter_context(nc.psum_tensor("bk%d" % i, [128, 512], F32)) for i in range(8)]

        def bkf(i):
            return banks[i][:]

        def bkb(i):
            return banks[i][:].bitcast(BF16)

        Win = sb("Win", [128, NL, 8, INW], BF16)
        Wout = sb("Wout", [128, NL, 8, D], BF16)
        ident = sb("ident", [128, 128], BF16)
        mask2 = sb("mask2", [128, 64], F32)
        wsT = sb("wsT", [128, NL, 4, 128], BF16)
        lnvg_bc = sb("lnvg_bc", [128, NL, 256])
        lnvb_bc = sb("lnvb_bc", [128, NL, 256])
        lng_bc = sb("lng_bc", [128, NL, D])
        lnb_bc = sb("lnb_bc", [128, NL, D])
        nbg_bc = sb("nbg_bc", [128, NL, 64])
        bsT = sb("bsT", [128, NL, 4])
        esink = sb("esink", [128, NL, 8])
        lbt = sb("lbt", [128, NL, 2])
        bco = sb("bco", [128, NL, 2])
        nbco = sb("nbco", [128, NL, 2])
        cosP = sb("cosP", [128, NT, 8])
        sinP = sb("sinP", [128, NT, 8])
        cosS = sb("cosS", [16, 8])
        sinS = sb("sinS", [16, 8])
        mhalf = sb("mhalf", [128, 4])
        ones_c = sb("ones_c", [128, 128])
        m1seg = sb("m1seg", [128, 32])

        xf = [sb("xf%d" % i, [128, D]) for i in range(2)]
        xb = [sb("xb%d" % i, [128, D], BF16) for i in range(2)]
        xT = sb("xT", [128, 8, 128], BF16)
        thg = sb("thg", [128, 1024])
        uf = sb("uf", [128, 256])
        st6 = sb("st6", [128, 2, 6])
        mv = sb("mv", [128, 2])
        veps = sb("veps", [128, 1])
        rstd = sb("rstd", [128, 1])
        vn0 = sb("vn0", [128, 256])
        vnb = sb("vnb", [128, 256], BF16)
        qb = sb("qb", [128, 512], BF16)
        rt = sb("rt", [128, 4, 8, 8])
        kf = sb("kf", [128, 128])
        vf = sb("vf", [128, 128])
        kdup = sb("kdup", [128, 2, 2, 64], BF16)
        NKV = 3
        vaug = [sb("vaug%d" % i, [128, 2, 65], BF16) for i in range(NKV * NL)]
        kT2 = [sb("kT2_%d" % i, [128, 2, 128], BF16) for i in range(NKV * NL)]
        qT = sb("qT", [128, 4, 128], BF16)
        pT = sb("pT", [128, 2, 8, 128], BF16)
        vb = sb("vb", [128, 256], BF16)
        thfq = sb("thfq", [128, 4, 128])
        Pf = sb("Pf", [128, 2, 128])
        Rf = sb("Rf", [128, 2, 128])
        scl = sb("scl", [128, 2, 2, 4])
        qt = sb("qt", [128, 2, 128], BF16)
        kt = sb("kt", [128, 2, 128], BF16)
        At = sb("At", [128, 4, 64], BF16)
        ktok = sb("ktok", [128, 256], BF16)
        Sp = sb("Sp", [128, 2, 2, 64], BF16)
        kvt = sb("kvt", [128, 2, 64])
        ss = sb("ss", [128, 4])
        ss2 = sb("ss2", [128, 4])
        rs = sb("rs", [128, 4])
        t1 = sb("t1", [128, 256])
        gB = sb("gB", [128, 256])
        ta = sb("ta", [128, 256])
        den = sb("den", [128, 8])
        rden = sb("rden", [128, 8])
        mix = sb("mix", [128, 1024], BF16)
        res = sb("res", [128, D])
        st12 = sb("st12", [128, 2, 6])
        mv2 = sb("mv2", [128, 2])
        veps2 = sb("veps2", [128, 1])
        rstd2 = sb("rstd2", [128, 1])
        nmr = sb("nmr", [128, 1])
        SstP = [[sb("S_0_%d_%d" % (s_, l), [128, 2, 64]) for l in range(NL)] for s_ in range(2)]
        SstS = [sb("S_1_%d" % l, [128, 2, 64]) for l in range(NL)]
        identf = t1[:, 0:128]
        smask = gB[:, 0:128]
        wsf = res[:, 0:512].rearrange("p (g j) -> p g j", g=4)
        wsb = mix[:, 0:512].rearrange("p (g j) -> p g j", g=4)
        mixT = xT
        qTf = thfq[:, 0:2, :]
        kTf = thfq[:, 2:4, :]
        tc_ = thfq[:].rearrange("p a b -> p (a b)")
        sq = ta
        ckf = t1[:, 0:128]
        cvf = t1[:, 128:256]
        ckb = kdup

        nunits = [0]
        uops = [0]
        KOPS = int(os.environ.get("KOPS", "100000000"))

        grp = [None]
        groups = {}

        def add(eng, fn, r=(), w=(), chan=None, rows=None):
            if grp[0] is None:
                return S.add(eng, fn, r=r, w=w, chan=chan, rows=rows)
            groups.setdefault(grp[0], []).append((eng, fn, tuple(r), tuple(w), chan, rows))

        ORDER = os.environ.get("KORDER", "in,gates,sguln,q,kv,fq_pe,attn_a,hg_dve,sgumix,hgrn_pe,attn_b,out,out_ln").split(",")
        DEFER = os.environ.get("KDEFER", "1") != "0"
        DEFER2 = os.environ.get("KDEFER2", "1") == "1"
        HG_FIRST = os.environ.get("KHGFIRST", "0") == "1"
        PREFETCH = os.environ.get("KPREF", "1") == "1"
        cur_pref = [None]
        pending_out = []
        pending_pref = []
        pending_ln_late = []
        LN_LATE = os.environ.get("KLNLATE", "1") == "1"
        pending_ln = []

        def emit_ops(lst):
            for (eng, fn, r, w, chan, rows) in lst:
                S.add(eng, fn, r=r, w=w, chan=chan, rows=rows)

        def drain_pending():
            emit_ops(pending_out)
            del pending_out[:]
            emit_ops(pending_pref)
            del pending_pref[:]
            emit_ops(pending_ln)
            del pending_ln[:]
            emit_ops(pending_ln_late)
            del pending_ln_late[:]

        def flush(defer, lyr=0):
            assert set(groups.keys()) <= set(ORDER), groups.keys()
            head = ("in", "gates", "q", "kv", "fq_pe", "sguln")
            if HG_FIRST:
                mid1 = ("hg_dve",)
                mid2 = ("attn_a",)
            else:
                mid1 = ()
                mid2 = ("attn_a", "hg_dve")
            rest = ("sgumix", "hgrn_pe", "attn_b")
            for g in head + mid1:
                emit_ops(groups.get(g, []))
            emit_ops(pending_out)
            del pending_out[:]
            if pending_pref:
                emit_ops(pending_pref)
                del pending_pref[:]
            emit_ops(pending_ln)
            del pending_ln[:]
            for g in mid2 + rest:
                emit_ops(groups.get(g, []))
            emit_ops(pending_ln_late)
            del pending_ln_late[:]
            if defer and DEFER:
                if PREFETCH and cur_pref[0] is not None:
                    pending_pref.append(cur_pref[0])
                pending_out.extend(groups.get("out", []))
                if LN_LATE and lyr == NL - 1:
                    pending_ln_late.extend(groups.get("out_ln", []))
                else:
                    pending_ln.extend(groups.get("out_ln", []))
            else:
                emit_ops(groups.get("out", []))
                emit_ops(groups.get("out_ln", []))
            groups.clear()
            grp[0] = None

        def wload(l):
            src = w_in[l].rearrange("(kc p) n -> p kc n", p=128)
            for (s0, d0, w) in W_SEGS:
                for k0 in range(0, 8, 2):
                    add("pool", lambda e, s0=s0, d0=d0, w=w, src=src, l=l, k0=k0: e.dma_start(out=Win[:, l, k0:k0 + 2, d0:d0 + w], in_=src[:, k0:k0 + 2, s0:s0 + w]),
                        w=["Win%d" % l], chan="w%d" % l)

        def woload(l):
            src = w_out[l].rearrange("(kc p) n -> p kc n", p=128)
            for h in range(2):
                for k0 in range(0, 8, 2):
                    add("pool", lambda e, h=h, src=src, l=l, k0=k0: e.dma_start(out=Wout[:, l, k0:k0 + 2, h * 512:(h + 1) * 512], in_=src[:, k0:k0 + 2, h * 512:(h + 1) * 512]),
                        w=["Wout%d" % l], chan="wo%d" % l)

        SKIP = os.environ.get("KSKIP", "")
        HLOAD = os.environ.get("KHLOAD", "1") == "1"

        def ld(dst, src, name):
            if "P" in SKIP and "bc" in name:
                return
            if "R" in SKIP and name in ("bsT", "lbt"):
                return
            add("sp", lambda e: e.dma_start(out=dst, in_=src), w=[name], chan="c_" + name)

        ld(mask2[:], mask2_d, "mask2")
        ld(smask, smask_d, "gB")
        ld(cosP[:], cosp, "cosP")
        ld(sinP[:], sinp, "sinP")
        ld(cosS[:], coss, "cosS")
        ld(sinS[:], sins, "sinS")
        for l in range(NL):
            ld(lnvg_bc[:, l, :], lnvg[l:l + 1, :].partition_broadcast(128), "lnvg_bc")
            ld(lnvb_bc[:, l, :], lnvb[l:l + 1, :].partition_broadcast(128), "lnvb_bc")
            ld(lng_bc[:, l, :], lng[l:l + 1, :].partition_broadcast(128), "lng_bc")
            ld(lnb_bc[:, l, :], lnb[l:l + 1, :].partition_broadcast(128), "lnb_bc")
            ld(nbg_bc[:, l, :], nbg[l:l + 1, :].partition_broadcast(128), "nbg_bc")
            ld(esink[:, l, :], sinks[l:l + 1, :].partition_broadcast(128), "esink")
        ld(bsT[:], b_s.rearrange("l g i -> i l g"), "bsT")
        ld(lbt[:], lbp.rearrange("l (h c) -> c l h", c=128), "lbt")
        add("act", lambda e: e.activation(out=esink[:], in_=esink[:], func=AF.Exp), r=["esink"], w=["esink"])
        add("pool", lambda e: e.memset(bco[:, 0, :], 0.5), w=["bco"])
        add("dve", lambda e: e.tensor_tensor(out=lbt[:, 0, :], in0=lbt[:, 1, :], in1=lbt[:, 0, :], op=ALU.subtract), r=["lbt"], w=["lbt"])
        add("act", lambda e: e.activation(out=lbt[:, 1, :], in_=lbt[:, 0, :], func=AF.Tanh, scale=0.5), r=["lbt"], w=["lbt"])
        add("dve", lambda e: e.tensor_scalar(out=bco[:, 1, :], in0=lbt[:, 1, :], scalar1=-0.25, scalar2=0.25, op0=ALU.mult, op1=ALU.add), r=["lbt", "bco"], w=["bco"])
        add("dve", lambda e: e.tensor_scalar(out=nbco[:], in0=bco[:], scalar1=-1.0, scalar2=None, op0=ALU.mult), r=["bco"], w=["nbco"])
        add("pool", lambda e: e.memset(mhalf[:], -0.5), w=["mhalf"])
        add("pool", lambda e: e.memset(ones_c[:], 1.0), w=["ones_c"])
        add("pool", lambda e: e.memset(m1seg[:], 0.0), w=["m1seg"])
        add("pool", lambda e: e.memset(m1seg[:, 0:1], 1.0), w=["m1seg"])
        add("pool", lambda e: e.memset(identf, 0.0), w=["t1"])
        add("pool", lambda e: e.affine_select(out=identf, in_=identf, pattern=[[-1, 128]], compare_op=ALU.not_equal, fill=1.0, base=0, channel_multiplier=1), r=["t1"], w=["t1"])
        add("dve", lambda e: e.tensor_copy(out=ident[:], in_=identf), r=["t1"], w=["ident"])
        add("pool", lambda e: e.memset(pT[:], 0.0), w=["pT"])
        for i in range(NKV * NL):
            add("pool", lambda e, i=i: e.memset(vaug[i][:], 1.0), w=["vaug%d" % i])
        for l in range(NL if "S" not in SKIP else 0):
            add("sp", lambda e, l=l: e.dma_start(out=wsf, in_=w_s[l].rearrange("g i j -> i g j")), w=["res0", "res1"], chan="c_wsf")
            add("dve", lambda e: e.tensor_tensor(out=wsb, in0=wsf, in1=smask.unsqueeze(1).to_broadcast([128, 4, 128]), op=ALU.mult), r=["res0", "res1", "gB"], w=["mixA", "mixB"])
            for g in range(4):
                add("pe", lambda e, g=g: e.transpose(out=bkb(2)[:, g * 128:(g + 1) * 128], in_=wsb[:, g, :], identity=ident[:]), r=["mixA", "mixB", "ident"], w=["B2"])
            add("act", lambda e, l=l: e.activation(out=wsT[:, l, :, :], in_=bkb(2)[:, 0:512].rearrange("p (g i) -> p g i", g=4), func=AF.Identity), r=["B2"], w=["wsT"])

        wload(0)
        if not HLOAD:
            woload(0)
            wload(1)
            woload(1)
        else:
            stg = [(res[:, 0:512], "res0"), (res[:, 512:1024], "res1"), (thg[:, 0:512], "thg0"), (thg[:, 512:1024], "thg1")]
            jobs = []
            for kc in range(8):
                for (s0, d0, w) in W_SEGS:
                    jobs.append((w_in[1][kc * 128:(kc + 1) * 128, s0:s0 + w], Win[:, 1, kc, d0:d0 + w], w, "Win1"))
            for kc in range(8):
                for h in range(2):
                    jobs.append((w_out[1][kc * 128:(kc + 1) * 128, h * 512:(h + 1) * 512], Wout[:, 1, kc, h * 512:(h + 1) * 512], 512, "Wout1"))
            for kc in range(8):
                for h in range(2):
                    jobs.append((w_out[0][kc * 128:(kc + 1) * 128, h * 512:(h + 1) * 512], Wout[:, 0, kc, h * 512:(h + 1) * 512], 512, "Wout0"))
            for ji, (src, dst, w, rname) in enumerate(jobs):
                buf, bname = stg[ji % len(stg)]
                add("sp", lambda e, src=src, buf=buf, w=w: e.dma_start(out=buf[:, 0:w], in_=src), w=[bname], chan="stg%d" % (ji % len(stg)))
                if ji % 2 == 0:
                    add("dve", lambda e, dst=dst, buf=buf, w=w: e.tensor_copy(out=dst, in_=buf[:, 0:w]), r=[bname], w=[rname])
                else:
                    add("act", lambda e, dst=dst, buf=buf, w=w: e.copy(out=dst, in_=buf[:, 0:w]), r=[bname], w=[rname])

        def layer_tile(U):
            grp[0] = "in"
            P = U["P"]
            l = U["l"]
            kind = U["kind"]
            s = U["s"]
            kd = 0 if kind == "p" else 1
            xslot = U["xslot"]
            nch = 2 if P == 128 else 1
            Sname = ("S_0_%d_%d" % (s, l)) if kd == 0 else ("S_1_%d" % l)
            St = SstP[s][l] if kd == 0 else SstS[l]
            cosT, sinT = U["cos"], U["sin"]
            cosn, sinn = U["cosn"], U["sinn"]
            xin = xf[xslot]
            xinn = "xf%d" % xslot
            xbt = xb[xslot]
            xbn = "xb%d" % xslot
            Wl = "Win%d" % l

            if l == 0:
                if not U.get("prefetched", False):
                    add("sp", lambda e: e.dma_start(out=xin[:P, :], in_=U["xsrc"]), w=[xinn], chan="x%d" % xslot)
                add("act", lambda e: e.copy(out=xbt[:P, :], in_=xin[:P, :]), r=[xinn], w=[xbn])
            for kc in range(8):
                add("pe", lambda e, kc=kc: e.transpose(out=bkb(2)[:, kc * 128:kc * 128 + P], in_=xbt[:P, kc * 128:(kc + 1) * 128], identity=ident[:P, :P]),
                    r=[xbn, "ident"], w=["B2"], rows=(0, P))
            add("act", lambda e: e.activation(out=xT[:, :, :P], in_=bkb(2).rearrange("p (a b) -> p a b", a=8)[:, :, :P], func=AF.Identity), r=["B2"], w=["xT"])
            grp[0] = "gates"


            def inproj(bank, col0):
                for kc in range(8):
                    add("pe", lambda e, kc=kc: e.matmul(bkf(bank)[:P, :], lhsT=xT[:, kc, :P], rhs=Win[:, l, kc, col0:col0 + 512], start=(kc == 0), stop=(kc == 7)),
                        r=["xT", Wl], w=["B%d" % bank])

            inproj(0, 0)
            add("act", lambda e: e.activation(out=thg[:P, 0:512], in_=bkf(0)[:P, :], func=AF.Tanh, scale=0.5), r=["B0"], w=["thg0"])
            add("dve", lambda e: e.scalar_tensor_tensor(out=thg[:P, 0:512], in0=thg[:P, 0:512], scalar=1.0, in1=bkf(0)[:P, :], op0=ALU.add, op1=ALU.mult), r=["thg0", "B0"], w=["thg0"])
            inproj(1, 512)
            add("act", lambda e: e.activation(out=thg[:P, 512:1024], in_=bkf(1)[:P, :], func=AF.Tanh, scale=0.5), r=["B1"], w=["thg1"])
            add("dve", lambda e: e.scalar_tensor_tensor(out=thg[:P, 512:1024], in0=thg[:P, 512:1024], scalar=1.0, in1=bkf(1)[:P, :], op0=ALU.add, op1=ALU.mult), r=["thg1", "B1"], w=["thg1"])
            grp[0] = "sguln"
            inproj(3, O_U)
            add("dve", lambda e: e.tensor_copy(out=uf[:P, :], in_=bkf(3)[:P, 0:256]), r=["B3"], w=["uf"])
            add("dve", lambda e: e.bn_stats(out=st6[:P, 0, :], in_=bkf(3)[:P, 256:512]), r=["B3"], w=["st6"])
            add("dve", lambda e: e.bn_aggr(out=mv[:P, :], in_=st6[:P, 0, :]), r=["st6"], w=["mv"])
            add("dve", lambda e: e.tensor_scalar(out=veps[:P, :], in0=mv[:P, 1:2], scalar1=LN_EPS, scalar2=None, op0=ALU.add), r=["mv"], w=["veps"])
            add("pool", lambda e: e.tensor_tensor(out=rstd[:P, :], in0=veps[:P, :], in1=mhalf[:P, 0:1], op=ALU.pow), r=["veps", "mhalf"], w=["rstd"])
            add("dve", lambda e: e.tensor_scalar(out=vn0[:P, :], in0=bkf(3)[:P, 256:512], scalar1=mv[:P, 0:1], scalar2=rstd[:P, 0:1], op0=ALU.subtract, op1=ALU.mult), r=["B3", "mv", "rstd"], w=["vn0"])
            add("dve", lambda e: e.tensor_tensor(out=vn0[:P, :], in0=vn0[:P, :], in1=lnvg_bc[:P, l, :], op=ALU.mult), r=["vn0", "lnvg_bc"], w=["vn0"])
            if kind == "p":
                add("dve", lambda e: e.tensor_tensor(out=vnb[:P, :], in0=vn0[:P, :], in1=lnvb_bc[:P, l, :], op=ALU.add), r=["vn0", "lnvb_bc"], w=["vnb"])
            else:
                add("dve", lambda e: e.tensor_tensor(out=vn0[:P, :], in0=vn0[:P, :], in1=lnvb_bc[:P, l, :], op=ALU.add), r=["vn0", "lnvb_bc"], w=["vn0"])
                add("act", lambda e: e.copy(out=vnb[:P, :], in_=vn0[:P, :]), r=["vn0"], w=["vnb"])
            if kind == "s":
                i = add("sp", lambda e: e.dma_start(out=nva[l, s], in_=vn0[:P, :]), r=["vn0"], w=["o%d" % uid[0]], chan="o_va")
                finals.append("o%d" % uid[0]); uid[0] += 1
            grp[0] = "q"
            inproj(4, O_QC)
            add("act", lambda e: e.activation(out=qb[:P, :], in_=bkf(4)[:P, :], func=AF.Identity), r=["B4"], w=["qb"])

            def rope(ps_view, out_view, nh, rname, wname):
                cb = cosT.unsqueeze(1).to_broadcast([P, nh, 8])
                sbb = sinT.unsqueeze(1).to_broadcast([P, nh, 8])
                x1 = ps_view[:, :, 0:8]
                x2 = ps_view[:, :, 8:16]
                add("dve", lambda e: e.tensor_tensor(out=rt[:P, 0, :nh, :], in0=x1, in1=cb, op=ALU.mult), r=[rname, cosn], w=["rt0"])
                add("dve", lambda e: e.tensor_tensor(out=rt[:P, 1, :nh, :], in0=x2, in1=sbb, op=ALU.mult), r=[rname, sinn], w=["rt1"])
                add("dve", lambda e: e.tensor_tensor(out=rt[:P, 2, :nh, :], in0=x2, in1=cb, op=ALU.mult), r=[rname, cosn], w=["rt2"])
                add("dve", lambda e: e.tensor_tensor(out=rt[:P, 3, :nh, :], in0=x1, in1=sbb, op=ALU.mult), r=[rname, sinn], w=["rt3"])
                add("dve", lambda e: e.tensor_tensor(out=out_view[:, :, 0:8], in0=rt[:P, 0, :nh, :], in1=rt[:P, 1, :nh, :], op=ALU.subtract), r=["rt0", "rt1"], w=[wname])
                add("dve", lambda e: e.tensor_tensor(out=out_view[:, :, 8:16], in0=rt[:P, 2, :nh, :], in1=rt[:P, 3, :nh, :], op=ALU.add), r=["rt2", "rt3"], w=[wname])

            rope(bkf(4)[:P, :].rearrange("p (h d) -> p h d", h=8), qb[:P, :].rearrange("p (h d) -> p h d", h=8), 8, "B4", "qb")
            grp[0] = "kv"
            inproj(5, O_KC)
            add("act", lambda e: e.activation(out=kf[:P, :], in_=bkf(5)[:P, 0:128], func=AF.Identity), r=["B5"], w=["kf"])
            rope(bkf(5)[:P, 0:128].rearrange("p (h d) -> p h d", h=2), kf[:P, :].rearrange("p (h d) -> p h d", h=2), 2, "B5", "kf")
            kvs = U["kvslot"]
            for dup in range(2):
                add("act", lambda e, dup=dup: e.copy(out=kdup[:P, :, dup, :], in_=kf[:P, :].rearrange("p (g d) -> p g d", g=2)), r=["kf"], w=["kdup"])
            add("act", lambda e: e.activation(out=vaug[kvs][:P, :, 0:64], in_=bkf(5)[:P, 128:256].rearrange("p (g d) -> p g d", g=2), func=AF.Identity), r=["B5"], w=["vaug%d" % kvs])
            add("act", lambda e: e.activation(out=vb[:P, :], in_=bkf(5)[:P, 256:512], func=AF.Identity), r=["B5"], w=["vb"])
            if U["kvout"] is not None:
                ko, vo = U["kvout"]
                add("act", lambda e: e.activation(out=vf[:P, :], in_=bkf(5)[:P, 128:256], func=AF.Identity), r=["B5"], w=["vf"])
                add("sp", lambda e: e.dma_start(out=ko, in_=kf[:P, :]), r=["kf"], w=["o%d" % uid[0]], chan="o_k")
                finals.append("o%d" % uid[0]); uid[0] += 1
                add("sp", lambda e: e.dma_start(out=vo, in_=vf[:P, :]), r=["vf"], w=["o%d" % uid[0]], chan="o_v")
                finals.append("o%d" % uid[0]); uid[0] += 1

            grp[0] = "fq_pe"
            fq = bkf(6).rearrange("p (a b) -> p a b", a=4)
            for j in range(4):
                col0 = (O_QB + 128 * j) if j < 2 else (O_FB + 128 * (j - 2))
                for kc in range(8):
                    add("pe", lambda e, kc=kc, j=j, col0=col0: e.matmul(fq[:, j, :P], lhsT=Win[:, l, kc, col0:col0 + 128], rhs=xT[:, kc, :P], start=(kc == 0), stop=(kc == 7)),
                        r=["xT", Wl], w=["B6"])
            add("act", lambda e: e.activation(out=thfq[:, :, :P], in_=fq[:, :, :P], func=AF.Tanh, scale=0.5), r=["B6"], w=["thfq_q", "thfq_k"])
            add("dve", lambda e: e.scalar_tensor_tensor(out=qTf[:, :, :P], in0=thfq[:, 0:2, :P], scalar=1.0, in1=fq[:, 0:2, :P], op0=ALU.add, op1=ALU.mult), r=["thfq_q", "B6"], w=["thfq_q"])
            grp[0] = "hg_dve"
            for h in range(2):
                add("dve", lambda e, h=h: e.tensor_scalar(out=kTf[:, h, :P], in0=thfq[:, 2 + h, :P], scalar1=nbco[:, l, h:h + 1], scalar2=bco[:, l, h:h + 1], op0=ALU.mult, op1=ALU.add),
                    r=["thfq_k", "bco", "nbco"], w=["thfq_k"])
            add("dve", lambda e: e.tensor_scalar(out=Rf[:, :, :P], in0=kTf[:, :, :P], scalar1=-1.0, scalar2=1.0, op0=ALU.mult, op1=ALU.add), r=["thfq_k"], w=["Rf"])
            if P == 128:
                rtf = rt[:].rearrange("p a b c -> p (a b c)")
                Rf8 = Rf[:].rearrange("p h (s j) -> p (h s) j", j=32)
                add("dve", lambda e: e.tensor_tensor(out=rtf.rearrange("p (s j) -> p s j", j=32), in0=Rf8, in1=m1seg[:, :].unsqueeze(1).to_broadcast([128, 8, 32]), op=ALU.mult),
                    r=["Rf", "m1seg"], w=["rt0", "rt1", "rt2", "rt3"])
                add("dve", lambda e: e.tensor_tensor(out=Rf[:].rearrange("p h t -> p (h t)"), in0=Rf[:].rearrange("p h t -> p (h t)"), in1=rtf, op=ALU.subtract), r=["Rf", "rt0", "rt1", "rt2", "rt3"], w=["Rf"])
                add("dve", lambda e: e.tensor_tensor_scan(out=Pf[:].rearrange("p h t -> p (h t)"), data0=Rf[:].rearrange("p h t -> p (h t)"), data1=rtf, initial=0.0, op0=ALU.mult, op1=ALU.add),
                    r=["Rf", "rt0", "rt1", "rt2", "rt3"], w=["Pf"])
                add("dve", lambda e: e.reciprocal(out=Rf[:, :, :], in_=Pf[:, :, :]), r=["Pf"], w=["Rf"])
                Pf4 = Pf[:].rearrange("p h (c j) -> p h c j", c=2)
                Rf4 = Rf[:].rearrange("p h (c j) -> p h c j", c=2)
                add("dve", lambda e: e.tensor_copy(out=scl[:, :, :, 0], in_=Pf4[:, :, :, 31]), r=["Pf"], w=["scl"])
                add("dve", lambda e: e.tensor_copy(out=scl[:, :, :, 1], in_=Rf4[:, :, :, 31]), r=["Rf"], w=["scl"])
                add("dve", lambda e: e.tensor_copy(out=scl[:, :, :, 3], in_=Pf4[:, :, :, 63]), r=["Pf"], w=["scl"])
                add("dve", lambda e: e.tensor_tensor(out=scl[:, :, :, 2], in0=scl[:, :, :, 0], in1=scl[:, :, :, 3], op=ALU.mult), r=["scl"], w=["scl"])
                add("dve", lambda e: e.tensor_tensor(out=Pf4[:, :, :, 0:32], in0=Pf4[:, :, :, 0:32], in1=scl[:, :, :, 1:2].to_broadcast([128, 2, 2, 32]), op=ALU.mult), r=["Pf", "scl"], w=["Pf"])
                add("dve", lambda e: e.tensor_tensor(out=Rf4[:, :, :, 0:32], in0=Rf4[:, :, :, 0:32], in1=scl[:, :, :, 0:1].to_broadcast([128, 2, 2, 32]), op=ALU.mult), r=["Rf", "scl"], w=["Rf"])
            else:
                segs = [(0, 16)]
                lastH1 = [15]
                lastH2 = [None]
                for h in range(2):
                    for (a, b) in segs:
                        add("dve", lambda e, h=h, a=a, b=b: e.tensor_tensor_scan(out=Pf[:, h, a:b], data0=Rf[:, h, a:b], data1=ones_c[:, a:b], initial=1.0, op0=ALU.mult, op1=ALU.mult),
                            r=["Rf", "ones_c"], w=["Pf"])
                add("dve", lambda e: e.reciprocal(out=Rf[:, :, :P], in_=Pf[:, :, :P]), r=["Pf"], w=["Rf"])
                for c in range(nch):
                    i1 = lastH1[c]
                    add("dve", lambda e, c=c, i1=i1: e.tensor_copy(out=scl[:, :, c, 0], in_=Pf[:, :, i1]), r=["Pf"], w=["scl"])
                    add("dve", lambda e, c=c, i1=i1: e.tensor_copy(out=scl[:, :, c, 1], in_=Rf[:, :, i1]), r=["Rf"], w=["scl"])
                    add("dve", lambda e, c=c: e.memset(scl[:, :, c, 3], 1.0), w=["scl"])
                    add("dve", lambda e, c=c: e.tensor_tensor(out=scl[:, :, c, 2], in0=scl[:, :, c, 0], in1=scl[:, :, c, 3], op=ALU.mult), r=["scl"], w=["scl"])
                for c in range(nch):
                    a = 64 * c
                    b = lastH1[c] + 1
                    for h in range(2):
                        add("dve", lambda e, c=c, h=h, a=a, b=b: e.tensor_scalar(out=Pf[:, h, a:b], in0=Pf[:, h, a:b], scalar1=scl[:, h, c, 1:2], scalar2=None, op0=ALU.mult), r=["Pf", "scl"], w=["Pf"])
                        add("dve", lambda e, c=c, h=h, a=a, b=b: e.tensor_scalar(out=Rf[:, h, a:b], in0=Rf[:, h, a:b], scalar1=scl[:, h, c, 0:1], scalar2=None, op0=ALU.mult), r=["Rf", "scl"], w=["Rf"])
            add("dve", lambda e: e.tensor_tensor(out=qt[:, :, :P], in0=qTf[:, :, :P], in1=Pf[:, :, :P], op=ALU.mult), r=["thfq_q", "Pf"], w=["qt"])
            add("dve", lambda e: e.tensor_tensor(out=kt[:, :, :P], in0=kTf[:, :, :P], in1=Rf[:, :, :P], op=ALU.mult), r=["thfq_k", "Rf"], w=["kt"])

            grp[0] = "sgumix"
            mixed = bkf(7)[:, 256:512]
            for g in range(4):
                add("pe", lambda e, g=g: e.matmul(mixed[:P, g * 64:(g + 1) * 64], lhsT=wsT[:P, l, g, :P], rhs=vnb[:P, g * 64:(g + 1) * 64], start=True, stop=True),
                    r=["wsT", "vnb"], w=["B7b"], rows=(0, P))
            add("dve", lambda e: e.tensor_tensor(out=ta[:P, :].rearrange("p (g d) -> p g d", g=4), in0=mixed[:P, :].rearrange("p (g d) -> p g d", g=4),
                                                 in1=bsT[:P, l, :].unsqueeze(2).to_broadcast([P, 4, 64]), op=ALU.add), r=["B7b", "bsT"], w=["ta"])
            add("dve", lambda e: e.tensor_tensor(out=ta[:P, :], in0=ta[:P, :], in1=uf[:P, :], op=ALU.mult), r=["ta", "uf"], w=["ta"])
            add("dve", lambda e: e.tensor_tensor(out=mix[:P, 0:256], in0=ta[:P, :], in1=thg[:P, 0:256], op=ALU.mult), r=["ta", "thg0"], w=["mixA"])

            grp[0] = "hgrn_pe"
            Aps = bkf(7)[:, 0:256].rearrange("p (h i) -> p h i", h=4)
            cl = min(64, P)
            for hh in range(2):
                for c in range(nch):
                    a = 64 * c
                    for hch in range(2):
                        h = 2 * hch + hh
                        add("pe", lambda e, a=a, cl=cl, h=h, hch=hch, hh=hh: e.matmul(Aps[a:a + cl, h, 0:cl], lhsT=kt[64 * hh:64 * hh + 64, hch, a:a + cl], rhs=qt[64 * hh:64 * hh + 64, hch, a:a + cl], start=True, stop=True),
                            r=["kt", "qt"], w=["B7a"], rows=(64 * hh, 64 * hh + 64))
            add("dve", lambda e: e.tensor_tensor(out=At[:P, :, 0:cl], in0=Aps[:P, :, 0:cl], in1=mask2[:P, 0:cl].unsqueeze(1).to_broadcast([P, 4, cl]), op=ALU.mult), r=["B7a", "mask2"], w=["At"])
            for h in range(2):
                add("pe", lambda e, h=h: e.transpose(out=bkb(2)[:P, h * 128:(h + 1) * 128], in_=kt[:, h, :P], identity=ident[:]), r=["kt", "ident"], w=["B2"])
            add("act", lambda e: e.activation(out=ktok[:P, :], in_=bkb(2)[:P, 0:256], func=AF.Identity), r=["B2"], w=["ktok"])
            ohg = bkf(7)[:, 256:512]
            kvps = [bkf(3 + c)[:, 288:416].rearrange("p (a b) -> p a b", a=2) for c in range(2)]
            for c in range(nch):
                a = 64 * c
                for h in range(4):
                    hch, hh = h // 2, h % 2
                    add("pe", lambda e, a=a, cl=cl, h=h, hch=hch, hh=hh, c=c: e.matmul(kvps[c][64 * hh:64 * hh + 64, hch, :], lhsT=ktok[a:a + cl, h * 64:(h + 1) * 64], rhs=vb[a:a + cl, h * 64:(h + 1) * 64], start=True, stop=True),
                        r=["ktok", "vb"], w=["B%db" % (3 + c)], rows=(a, a + cl))
            for c in range(nch):
                add("dve", lambda e, c=c: e.tensor_tensor(out=Sp[:, c, :, :], in0=St[:, :, :], in1=scl[:, :, c, 0:1].to_broadcast([128, 2, 64]), op=ALU.mult), r=[Sname, "scl"], w=["Sp%d" % c])
                add("dve", lambda e, c=c: e.tensor_tensor(out=kvt[:, :, :], in0=kvps[c][:, :, :], in1=scl[:, :, c, 3:4].to_broadcast([128, 2, 64]), op=ALU.mult), r=["B%db" % (3 + c), "scl"], w=["kvt"])
                add("dve", lambda e, c=c: e.tensor_tensor(out=St[:, :, :], in0=St[:, :, :], in1=scl[:, :, c, 2:3].to_broadcast([128, 2, 64]), op=ALU.mult), r=[Sname, "scl"], w=[Sname])
                add("dve", lambda e, c=c: e.tensor_tensor(out=St[:, :, :], in0=St[:, :, :], in1=kvt[:, :, :], op=ALU.add), r=[Sname, "kvt"], w=[Sname])
            for c in range(nch):
                a = 64 * c
                for h in range(4):
                    hch, hh = h // 2, h % 2
                    add("pe", lambda e, a=a, cl=cl, h=h: e.matmul(ohg[a:a + cl, h * 64:(h + 1) * 64], lhsT=At[a:a + cl, h, 0:cl], rhs=vb[a:a + cl, h * 64:(h + 1) * 64], start=True, stop=False),
                        r=["At", "vb"], w=["B7b"], rows=(a, a + cl))
                    add("pe", lambda e, a=a, cl=cl, h=h, hch=hch, hh=hh, c=c: e.matmul(ohg[a:a + cl, h * 64:(h + 1) * 64], lhsT=qt[64 * hh:64 * hh + 64, hch, a:a + cl], rhs=Sp[64 * hh:64 * hh + 64, c, hch, :], start=False, stop=True),
                        r=["qt", "Sp%d" % c], w=["B7b"], rows=(64 * hh, 64 * hh + 64))
            if U["Sout"] is not None:
                add("sp", lambda e: e.dma_start(out=U["Sout"].rearrange("(a b) k v -> (b k) a v", a=2), in_=St[:]), r=[Sname], w=["o%d" % uid[0]], chan="o_S")
                finals.append("o%d" % uid[0]); uid[0] += 1
            add("act", lambda e: e.activation(out=sq[:P, :], in_=ohg[:P, :], func=AF.Square), r=["B7b"], w=["ta"])
            add("dve", lambda e: e.tensor_reduce(out=ss[:P, :], in_=sq[:P, :].rearrange("p (h v) -> p h v", h=4), axis=AX, op=ALU.add), r=["ta"], w=["ss"])
            add("dve", lambda e: e.tensor_scalar(out=ss2[:P, :], in0=ss[:P, :], scalar1=1.0 / 64.0, scalar2=4.0 * RMS_EPS, op0=ALU.mult, op1=ALU.add), r=["ss"], w=["ss2"])
            add("pool", lambda e: e.tensor_tensor(out=rs[:P, :], in0=ss2[:P, :], in1=mhalf[:P, :], op=ALU.pow), r=["ss2", "mhalf"], w=["rs"])
            add("dve", lambda e: e.tensor_tensor(out=t1[:P, :].rearrange("p (h v) -> p h v", h=4), in0=ohg[:P, :].rearrange("p (h v) -> p h v", h=4), in1=rs[:P, :].unsqueeze(2).to_broadcast([P, 4, 64]), op=ALU.mult), r=["B7b", "rs"], w=["t1"])
            add("dve", lambda e: e.tensor_tensor(out=gB[:P, :].rearrange("p (h v) -> p h v", h=4), in0=thg[:P, 256:512].rearrange("p (h v) -> p h v", h=4), in1=nbg_bc[:P, l, :].unsqueeze(1).to_broadcast([P, 4, 64]), op=ALU.mult), r=["thg0", "nbg_bc"], w=["gB"])
            add("dve", lambda e: e.tensor_tensor(out=mix[:P, 256:512], in0=t1[:P, :], in1=gB[:P, :], op=ALU.mult), r=["t1", "gB"], w=["mixB"])

            grp[0] = "attn_a"
            for blk in range(4):
                add("pe", lambda e, blk=blk: e.transpose(out=bkb(2)[:, blk * 128:blk * 128 + P], in_=qb[:P, blk * 128:(blk + 1) * 128], identity=ident[:P, :P]), r=["qb", "ident"], w=["B2"], rows=(0, P))
            for g in range(2):
                add("pe", lambda e, g=g: e.transpose(out=bkb(2)[:, 512 + g * 128:512 + g * 128 + P], in_=kdup[:P, g, :, :].rearrange("p a d -> p (a d)"), identity=ident[:P, :P]), r=["kdup", "ident"], w=["B2"], rows=(0, P))
            add("act", lambda e: e.activation(out=qT[:, :, :P], in_=bkb(2)[:, 0:512].rearrange("p (a b) -> p a b", a=4)[:, :, :P], func=AF.Identity), r=["B2"], w=["qT"])
            add("act", lambda e: e.activation(out=kT2[kvs][:, :, :P], in_=bkb(2)[:, 512:768].rearrange("p (a b) -> p a b", a=2)[:, :, :P], func=AF.Identity), r=["B2"], w=["kT2_%d" % kvs])
            ktl = []
            if U["prev"] is not None:
                ktl.append((U["prev"], 128, 0))
            ktl.append((kvs, P, 1))
            sctr = 0
            pT5 = pT[:].rearrange("p k (j t) q -> p k j t q", t=2)
            for (slot, nk, kti) in ktl:
                for hh in range(2):
                    sbank = sctr % 2
                    sctr += 1
                    sps = bkf(sbank).rearrange("p (h q) -> p h q", h=4)
                    sname = "B%d" % sbank
                    for jh in range(4):
                        g = jh // 2
                        add("pe", lambda e, slot=slot, nk=nk, jh=jh, hh=hh, g=g, sps=sps: e.matmul(sps[:nk, jh, :P], lhsT=kT2[slot][64 * hh:64 * hh + 64, g, :nk], rhs=qT[64 * hh:64 * hh + 64, jh, :P], start=True, stop=True),
                            r=["kT2_%d" % slot, "qT"], w=[sname], rows=(64 * hh, 64 * hh + 64))
                    if kind == "p":
                        if kti == 0:
                            regs = [(0, 128, 0, 64), (64, 128, 64, 128)]
                        else:
                            regs = [(0, 64, 0, 64), (0, 128, 64, 128)]
                    else:
                        regs = [(0, nk, 0, P)]
                    for (k0, k1, q0, q1) in regs:
                        add("act", lambda e, kti=kti, hh=hh, k0=k0, k1=k1, q0=q0, q1=q1, sps=sps: e.activation(out=pT5[k0:k1, kti, :, hh, q0:q1], in_=sps[k0:k1, :, q0:q1], func=AF.Exp, scale=0.125),
                            r=[sname], w=["pT"])
            oat = [bkf(3)[:, 0:260].rearrange("p (h d) -> p h d", h=4), bkf(4)[:, 0:260].rearrange("p (h d) -> p h d", h=4)]
            for h in range(8):
                hg, hl = h // 4, h % 4
                g = h // 4
                for i, (slot, nk, kti) in enumerate(ktl):
                    add("pe", lambda e, slot=slot, nk=nk, kti=kti, h=h, hg=hg, hl=hl, g=g, i=i: e.matmul(oat[hg][:P, hl, :], lhsT=pT[:nk, kti, h, :P], rhs=vaug[slot][:nk, g, :], start=(i == 0), stop=(i == len(ktl) - 1)),
                        r=["pT", "vaug%d" % slot], w=["B%da" % (3 + hg)], rows=(0, nk))
            grp[0] = "attn_b"
            for hg in range(2):
                bn = "B%da" % (3 + hg)
                add("dve", lambda e, hg=hg: e.tensor_tensor(out=den[:P, hg * 4:hg * 4 + 4], in0=oat[hg][:P, :, 64], in1=esink[:P, l, hg * 4:hg * 4 + 4], op=ALU.add), r=[bn, "esink"], w=["den%d" % hg])
                add("dve", lambda e, hg=hg: e.reciprocal(out=rden[:P, hg * 4:hg * 4 + 4], in_=den[:P, hg * 4:hg * 4 + 4]), r=["den%d" % hg], w=["rden%d" % hg])
                add("dve", lambda e, hg=hg: e.tensor_tensor(out=tc_[:P, hg * 256:(hg + 1) * 256].rearrange("p (h d) -> p h d", h=4), in0=oat[hg][:P, :, 0:64], in1=rden[:P, hg * 4:hg * 4 + 4].unsqueeze(2).to_broadcast([P, 4, 64]), op=ALU.mult),
                    r=[bn, "rden%d" % hg], w=["thfq_q" if hg == 0 else "thfq_k"])
                add("dve", lambda e, hg=hg: e.tensor_tensor(out=mix[:P, 512 + hg * 256:512 + (hg + 1) * 256], in0=tc_[:P, hg * 256:(hg + 1) * 256], in1=thg[:P, 512 + hg * 256:512 + (hg + 1) * 256], op=ALU.mult),
                    r=["thfq_q" if hg == 0 else "thfq_k", "thg1"], w=["mixC%d" % hg])

            grp[0] = "out"
            for kc in range(8):
                add("pe", lambda e, kc=kc: e.transpose(out=bkb(2)[:, kc * 128:kc * 128 + P], in_=mix[:P, kc * 128:(kc + 1) * 128], identity=ident[:P, :P]),
                    r=["mixA", "mixB", "mixC0", "mixC1", "ident"], w=["B2"], rows=(0, P))
            add("act", lambda e: e.activation(out=mixT[:, :, :P], in_=bkb(2).rearrange("p (a b) -> p a b", a=8)[:, :, :P], func=AF.Identity), r=["B2"], w=["xT"])
            for nb in range(2):
                for kc in range(8):
                    add("pe", lambda e, kc=kc, nb=nb: e.matmul(bkf(nb)[:P, :], lhsT=mixT[:, kc, :P], rhs=Wout[:, l, kc, nb * 512:(nb + 1) * 512], start=(kc == 0), stop=(kc == 7)),
                        r=["xT", "Wout%d" % l], w=["B%d" % nb])
                add("dve", lambda e, nb=nb: e.scalar_tensor_tensor(out=res[:P, nb * 512:(nb + 1) * 512], in0=bkf(nb)[:P, :], scalar=0.5 / ALPHA, in1=xin[:P, nb * 512:(nb + 1) * 512], op0=ALU.mult, op1=ALU.add),
                    r=["B%d" % nb, xinn], w=["res%d" % nb])
                add("dve", lambda e, nb=nb: e.bn_stats(out=st12[:P, nb, :], in_=res[:P, nb * 512:(nb + 1) * 512]), r=["res%d" % nb], w=["st12_%d" % nb])
            if dbg is not None and kind == "p" and dbg["sel"] == (s, U["t"], l):
                add("sp", lambda e: e.dma_start(out=dbg["mix"], in_=mix[:, :]), r=["mixA", "mixB", "mixC0", "mixC1"], w=["odbg0"], chan="o_dbg")
                add("sp", lambda e: e.dma_start(out=dbg["res"], in_=res[:, :]), r=["res0", "res1"], w=["odbg1"], chan="o_dbg")
                add("sp", lambda e: e.dma_start(out=dbg["thg"], in_=thg[:, :]), r=["thg0", "thg1"], w=["odbg2"], chan="o_dbg")
                finals.extend(["odbg0", "odbg1", "odbg2"])
            grp[0] = "out_ln"
            add("dve", lambda e: e.bn_aggr(out=mv2[:P, :], in_=st12[:P, :, :].rearrange("p a b -> p (a b)")), r=["st12_0", "st12_1"], w=["mv2"])
            add("dve", lambda e: e.tensor_scalar(out=veps2[:P, :], in0=mv2[:P, 1:2], scalar1=LN_EPS / (ALPHA * ALPHA), scalar2=None, op0=ALU.add), r=["mv2"], w=["veps2"])
            add("pool", lambda e: e.tensor_tensor(out=rstd2[:P, :], in0=veps2[:P, :], in1=mhalf[:P, 0:1], op=ALU.pow), r=["veps2", "mhalf"], w=["rstd2"])
            add("dve", lambda e: e.scalar_tensor_tensor(out=nmr[:P, :], in0=mv2[:P, 0:1], scalar=-1.0, in1=rstd2[:P, :], op0=ALU.mult, op1=ALU.mult), r=["mv2", "rstd2"], w=["nmr"])
            add("dve", lambda e: e.tensor_scalar(out=res[:P, :], in0=res[:P, :], scalar1=rstd2[:P, 0:1], scalar2=nmr[:P, 0:1], op0=ALU.mult, op1=ALU.add), r=["res0", "res1", "nmr", "rstd2"], w=["res0", "res1"])
            add("dve", lambda e: e.tensor_tensor(out=res[:P, :], in0=res[:P, :], in1=lng_bc[:P, l, :], op=ALU.mult), r=["res0", "res1", "lng_bc"], w=["res0", "res1"])
            if l == 0:
                add("dve", lambda e: e.tensor_tensor(out=xin[:P, :], in0=res[:P, :], in1=lnb_bc[:P, l, :], op=ALU.add), r=["res0", "res1", "lnb_bc"], w=[xinn])
                add("act", lambda e: e.copy(out=xbt[:P, :], in_=xin[:P, :]), r=[xinn], w=[xbn])
            else:
                add("dve", lambda e: e.tensor_tensor(out=res[:P, :], in0=res[:P, :], in1=lnb_bc[:P, l, :], op=ALU.add), r=["res0", "res1", "lnb_bc"], w=["res0", "res1"])
                add("sp", lambda e: e.dma_start(out=U["ydst"], in_=res[:P, :]), r=["res0", "res1"], w=["o%d" % uid[0]], chan="o_y")
                finals.append("o%d" % uid[0]); uid[0] += 1
            flush(kind == "p", l)

        kvctr = [0, 0]
        for s in range(2):
            for l in range(NL):
                add("pool", lambda e, s=s, l=l: e.memset(SstP[s][l][:], 0.0), w=["S_0_%d_%d" % (s, l)])
        prevs = [[None, None], [None, None]]
        for t in range(NT):
            for l in range(NL):
                for s in range(2):
                    slot = l * NKV + (kvctr[l] % NKV)
                    kvctr[l] += 1
                    U = dict(P=128, l=l, kind="p", s=s, xslot=s, t=t,
                             cos=cosP[:, t, :], sin=sinP[:, t, :], cosn="cosP", sinn="sinP",
                             xsrc=xp[s, t * 128:(t + 1) * 128, :],
                             kvslot=slot, prev=prevs[s][l],
                             kvout=(nkp[l, s], nvp[l, s]) if t == NT - 1 else None,
                             Sout=nhp[l, s] if t == NT - 1 else None,
                             ydst=yp[s, t * 128:(t + 1) * 128, :])
                    U["prefetched"] = PREFETCH and DEFER and l == 0 and t > 0
                    if l == 1 and t + 1 < NT:
                        cur_pref[0] = ("sp", (lambda e, s=s, t=t: e.dma_start(out=xf[s][:, :], in_=xp[s, (t + 1) * 128:(t + 2) * 128, :])),
                                       (), ("xf%d" % s,), "x%d" % s, None)
                    else:
                        cur_pref[0] = None
                    layer_tile(U)
                    prevs[s][l] = slot

        drain_pending()
        for s in range(2):
            xslot = s
            for l in range(NL):
                slotc = l * NKV + (kvctr[l] % NKV)
                kvctr[l] += 1
                slotn = l * NKV + (kvctr[l] % NKV)
                kvctr[l] += 1
                add("sp", lambda e, s=s, l=l: e.dma_start(out=ckf, in_=ck[l, s]), w=["t1"], chan="c_ckf")
                add("sp", lambda e, s=s, l=l: e.dma_start(out=cvf, in_=cv[l, s]), w=["t1"], chan="c_ckf")
                for dup in range(2):
                    add("act", lambda e, dup=dup: e.copy(out=ckb[:, :, dup, :], in_=ckf.rearrange("p (g d) -> p g d", g=2)), r=["t1"], w=["kdup"])
                for g in range(2):
                    add("pe", lambda e, g=g: e.transpose(out=bkb(2)[:, g * 128:(g + 1) * 128], in_=ckb[:, g, :, :].rearrange("p a d -> p (a d)"), identity=ident[:]), r=["kdup", "ident"], w=["B2"])
                add("act", lambda e, slotc=slotc: e.activation(out=kT2[slotc][:, :, :], in_=bkb(2)[:, 0:256].rearrange("p (a b) -> p a b", a=2), func=AF.Identity), r=["B2"], w=["kT2_%d" % slotc])
                add("act", lambda e, slotc=slotc: e.copy(out=vaug[slotc][:, :, 0:64], in_=cvf.rearrange("p (g d) -> p g d", g=2)), r=["t1"], w=["vaug%d" % slotc])
                add("sp", lambda e, s=s, l=l: e.dma_start(out=SstS[l][:], in_=st_in[l, s].rearrange("(a b) k v -> (b k) a v", a=2)), w=["S_1_%d" % l], chan="c_S%d" % l)
                cur_pref[0] = None
                U = dict(P=16, l=l, kind="s", s=s, xslot=xslot, t=0,
                         cos=cosS[:, :], sin=sinS[:, :], cosn="cosS", sinn="sinS",
                         xsrc=xs[s], kvslot=slotn, prev=slotc,
                         kvout=(nks[l, s], nvs[l, s]), Sout=nhs[l, s], ydst=ys[s])
                layer_tile(U)
        add("sp", None, r=finals)
        with nc.allow_non_contiguous_dma(reason="tiny param transposes"):
            S.emit(nc, stack)
    return nc, S


_CACHE = {}
_LAST = None


def _rope_tables(pos):
    half = 8
    inv = np.power(np.float32(500000.0), -np.arange(0, 16, 2, dtype=np.float32) / np.float32(16)).astype(np.float32)
    ang = pos.astype(np.float32)[:, None] * inv[None, :]
    return np.cos(ang).astype(np.float32), np.sin(ang).astype(np.float32)


def kernel(x_prompt, x_sample, cache_k, cache_v, state_hgrn, w_in, ln_v_g, ln_v_b, w_s, b_s,
           lb_param, norm_b_g, sinks, w_out, ln_g, ln_b):
    f = lambda a: np.ascontiguousarray(np.asarray(a, dtype=np.float32))
    x_prompt = f(x_prompt); x_sample = f(x_sample)
    B, T, _ = x_prompt.shape
    NT = T // 128
    if NT not in _CACHE:
        _CACHE[NT] = build(NT)
    nc, S = _CACHE[NT]
    cp, sp_ = _rope_tables(np.arange(T))
    cosp = np.ascontiguousarray(cp.reshape(NT, 128, 8).transpose(1, 0, 2))
    sinp = np.ascontiguousarray(sp_.reshape(NT, 128, 8).transpose(1, 0, 2))
    cs, ss_ = _rope_tables(PAST + np.arange(16))
    p = np.arange(128)
    mask2 = ((p[:, None] % 64) <= np.arange(64)[None, :]).astype(np.float32)
    smask = ((p[None, :] // 64) <= (p[:, None] // 64)).astype(np.float32)
    ck = f(cache_k).reshape(NL, B, 128, 128)
    cv = f(cache_v).reshape(NL, B, 128, 128)
    st = f(state_hgrn)
    shared = dict(w_in=f(w_in), w_out=f(w_out), lnvg=f(ln_v_g), lnvb=f(ln_v_b), w_s=f(w_s), b_s=f(b_s),
                  lbp=f(lb_param), nbg=f(norm_b_g), sinks=f(sinks), lng=f(ln_g), lnb=f(ln_b),
                  cosp=cosp, sinp=sinp, coss=cs, sins=ss_, mask2=mask2, smask=smask)
    in_maps = []
    for c in range(NCORES):
        m = dict(shared)
        m["xp"] = np.ascontiguousarray(x_prompt[2 * c:2 * c + 2])
        m["xs"] = np.ascontiguousarray(x_sample[2 * c:2 * c + 2])
        m["ck"] = np.ascontiguousarray(ck[:, 2 * c:2 * c + 2])
        m["cv"] = np.ascontiguousarray(cv[:, 2 * c:2 * c + 2])
        m["st"] = np.ascontiguousarray(st[:, 2 * c:2 * c + 2])
        in_maps.append(m)
    res = run_bass_kernel_spmd(nc, in_maps, core_ids=list(range(NCORES)))
    R = res.results
    global _LAST
    _LAST = R

    def cat(name, axis):
        return np.concatenate([np.asarray(r[name], dtype=np.float32) for r in R], axis=axis)

    y_p = cat("yp", 0)
    y_s = cat("ys", 0)
    nkp = cat("nkp", 1).reshape(NL, B, 128, 2, 64)
    nvp = cat("nvp", 1).reshape(NL, B, 128, 2, 64)
    nhp = cat("nhp", 1)
    nks = cat("nks", 1).reshape(NL, B, 16, 2, 64)
    nvs = cat("nvs", 1).reshape(NL, B, 16, 2, 64)
    nhs = cat("nhs", 1)
    nva = cat("nva", 1)
    return (y_p, y_s, nkp, nvp, nhp, nks, nvs, nhs, nva)
```
